# Optimizing a Trainium2 kernel written in Bass

```python
import jax, jax.numpy as jnp
from jax import lax
import numpy as np

D_MODEL = 2048
BATCH = 4
SEQ = 2048
DEPTH = 1
DEC_BATCH = 16
DEC_SEQ = 2048
PAST_LEN = 128

HEAD_DIM = 128
N_Q_HEADS = D_MODEL // HEAD_DIM
N_KV_HEADS = N_Q_HEADS // 4
Q_PER_KV = N_Q_HEADS // N_KV_HEADS
ATTN_WIDTH = N_Q_HEADS * HEAD_DIM
KV_WIDTH = N_KV_HEADS * HEAD_DIM
N_FOURIER_GROUPS = 4
FOURIER_WIDTH = D_MODEL // 2
FOURIER_GROUP_DIM = FOURIER_WIDTH // N_FOURIER_GROUPS
IN_WIDTH = ATTN_WIDTH + 2 * KV_WIDTH + FOURIER_WIDTH
WINDOW = 128
BLOCK = 128
D_FF = -(-8 * D_MODEL // (3 * 256)) * 256
N_MOD = 6
RMS_EPS = 1e-6

kernel_name = "hybrid_gated_fnet_swa_encoder"


def rmsnorm(x, g):
    xf = x.astype(jnp.float32)
    y = xf * lax.rsqrt(jnp.mean(xf * xf, axis=-1, keepdims=True) + RMS_EPS)
    return (y * g.astype(jnp.float32)).astype(x.dtype)


def alibi_slopes():
    h = jnp.arange(1, N_Q_HEADS + 1, dtype=jnp.float32)
    return jnp.exp2(-8.0 * h / N_Q_HEADS).reshape(N_KV_HEADS, Q_PER_KV)


def banded_window_attention(q, k, v, sink):
    B, S = q.shape[0], q.shape[1]
    nb = S // BLOCK
    pad = ((0, 0), (BLOCK, BLOCK), (0, 0), (0, 0))
    kp = jnp.pad(k, pad)
    vp = jnp.pad(v, pad)
    qb = jnp.moveaxis(q.reshape(B, nb, BLOCK, N_KV_HEADS, Q_PER_KV, HEAD_DIM), 1, 0)
    a = jnp.arange(BLOCK)[:, None]
    j = jnp.arange(3 * BLOCK)[None, :]
    rel = j - BLOCK - a
    dist = jnp.abs(rel).astype(jnp.float32)
    in_window = jnp.abs(rel) <= WINDOW
    slopes = alibi_slopes()
    bias = -slopes[:, :, None, None] * dist[None, None]
    sink_l = sink.astype(jnp.float32).reshape(N_KV_HEADS, Q_PER_KV)
    scale = HEAD_DIM ** -0.5

    def one_block(args):
        i, qi = args
        start = i * BLOCK
        ki = lax.dynamic_slice_in_dim(kp, start, 3 * BLOCK, axis=1)
        vi = lax.dynamic_slice_in_dim(vp, start, 3 * BLOCK, axis=1)
        key_pos = start - BLOCK + jnp.arange(3 * BLOCK)
        valid = in_window & ((key_pos >= 0) & (key_pos < S))[None, :]
        s = jnp.einsum('bqkgd,bskd->bkgqs', qi, ki,
                       preferred_element_type=jnp.float32) * scale + bias
        s = jnp.where(valid, s, -jnp.inf)
        sink_col = jnp.broadcast_to(sink_l[None, :, :, None, None], s.shape[:-1] + (1,))
        p = jax.nn.softmax(jnp.concatenate([s, sink_col], axis=-1), axis=-1)[..., :-1]
        return jnp.einsum('bkgqs,bskd->bqkgd', p.astype(vi.dtype), vi)

    out = lax.map(one_block, (jnp.arange(nb), qb))
    return jnp.moveaxis(out, 0, 1).reshape(B, S, ATTN_WIDTH)


def fourier_mix(u):
    B, S = u.shape[0], u.shape[1]
    ug = u.astype(jnp.float32).reshape(B, S, N_FOURIER_GROUPS, FOURIER_GROUP_DIM)
    z = jnp.fft.fft2(ug, axes=(1, 3), norm="ortho")
    return jnp.real(z).reshape(B, S, FOURIER_WIDTH).astype(u.dtype)


def encoder_layer(x, c, w_mod, b_mod, g_mix, w_in, attn_sink, w_attn_branch,
                  w_fourier_branch, w_gate, b_gate, w_out, g_ffn, w_up, w_down):
    B, S = x.shape[0], x.shape[1]
    mod = (jax.nn.silu(c) @ w_mod + b_mod)[:, None, :]
    shift1, scale1, gate1, shift2, scale2, gate2 = jnp.split(mod, N_MOD, axis=-1)

    h = rmsnorm(x, g_mix) * (1.0 + scale1) + shift1
    proj = h @ w_in
    q, k, v, u = jnp.split(proj, [ATTN_WIDTH, ATTN_WIDTH + KV_WIDTH,
                                  ATTN_WIDTH + 2 * KV_WIDTH], axis=-1)
    q = q.reshape(B, S, N_KV_HEADS, Q_PER_KV, HEAD_DIM)
    k = k.reshape(B, S, N_KV_HEADS, HEAD_DIM)
    v = v.reshape(B, S, N_KV_HEADS, HEAD_DIM)
    attn = banded_window_attention(q, k, v, attn_sink) @ w_attn_branch
    four = fourier_mix(u) @ w_fourier_branch
    g_attn, g_four = jnp.split(jax.nn.sigmoid(h @ w_gate + b_gate), 2, axis=-1)
    merged = g_attn * attn + g_four * four
    x = x + gate1 * (merged @ w_out)

    h2 = rmsnorm(x, g_ffn) * (1.0 + scale2) + shift2
    gt, up = jnp.split(h2 @ w_up, 2, axis=-1)
    x = x + gate2 * ((jax.nn.silu(gt) * up) @ w_down)
    return x


def trunk(x, c, w_mod, b_mod, g_mix, w_in, attn_sink, w_attn_branch, w_fourier_branch,
          w_gate, b_gate, w_out, g_ffn, w_up, w_down, g_final):
    for l in range(DEPTH):
        x = encoder_layer(x, c, w_mod[l], b_mod[l], g_mix[l], w_in[l], attn_sink[l],
                          w_attn_branch[l], w_fourier_branch[l], w_gate[l], b_gate[l],
                          w_out[l], g_ffn[l], w_up[l], w_down[l])
    return rmsnorm(x, g_final)


def setup_inputs(seed: int = 0) -> dict:
    key = jax.random.key(seed)
    ks = jax.random.split(key, 20)
    f32 = jnp.float32

    def dense(k, fan_in, fan_out, gain=1.0):
        return jax.random.normal(k, (DEPTH, fan_in, fan_out), f32) * (gain * fan_in ** -0.5)

    def gain_vec(k, n):
        return 1.0 + 0.02 * jax.random.normal(k, (DEPTH, n), f32)

    return {
        "x_prompt": jax.random.normal(ks[0], (BATCH, SEQ, D_MODEL), f32),
        "x_sample": jax.random.normal(ks[1], (DEC_BATCH, DEC_SEQ, D_MODEL), f32),
        "c_prompt": jax.random.normal(ks[2], (BATCH, D_MODEL), f32),
        "c_sample": jax.random.normal(ks[3], (DEC_BATCH, D_MODEL), f32),
        "w_mod": dense(ks[4], D_MODEL, N_MOD * D_MODEL, 0.5),
        "b_mod": 0.02 * jax.random.normal(ks[5], (DEPTH, N_MOD * D_MODEL), f32),
        "g_mix": gain_vec(ks[6], D_MODEL),
        "w_in": dense(ks[7], D_MODEL, IN_WIDTH),
        "attn_sink": 0.5 * jax.random.normal(ks[8], (DEPTH, N_Q_HEADS), f32),
        "w_attn_branch": dense(ks[9], ATTN_WIDTH, D_MODEL),
        "w_fourier_branch": dense(ks[10], FOURIER_WIDTH, D_MODEL),
        "w_gate": dense(ks[11], D_MODEL, 2 * D_MODEL),
        "b_gate": 0.02 * jax.random.normal(ks[12], (DEPTH, 2 * D_MODEL), f32),
        "w_out": dense(ks[13], D_MODEL, D_MODEL),
        "g_ffn": gain_vec(ks[14], D_MODEL),
        "w_up": dense(ks[15], D_MODEL, 2 * D_FF),
        "w_down": dense(ks[16], D_FF, D_MODEL),
        "g_final": 1.0 + 0.02 * jax.random.normal(ks[17], (D_MODEL,), f32),
    }


def reference(x_prompt, x_sample, c_prompt, c_sample, w_mod, b_mod, g_mix, w_in,
              attn_sink, w_attn_branch, w_fourier_branch, w_gate, b_gate, w_out,
              g_ffn, w_up, w_down, g_final):
    y_prompt = trunk(x_prompt, c_prompt, w_mod, b_mod, g_mix, w_in, attn_sink,
                     w_attn_branch, w_fourier_branch, w_gate, b_gate, w_out,
                     g_ffn, w_up, w_down, g_final)
    y_sample = trunk(x_sample, c_sample, w_mod, b_mod, g_mix, w_in, attn_sink,
                     w_attn_branch, w_fourier_branch, w_gate, b_gate, w_out,
                     g_ffn, w_up, w_down, g_final)
    return (y_prompt, y_sample)
```

```python
from contextlib import ExitStack
import numpy as np
import ml_dtypes
import concourse.bass as bass
import concourse.mybir as mybir
from concourse.bass_utils import run_bass_kernel_spmd

F32 = mybir.dt.float32
BF = mybir.dt.bfloat16
AF = mybir.ActivationFunctionType
ALU = mybir.AluOpType

D = 2048
S = 2048
NH = 16
DFF = 5632
NCORES = 8
EPS = 1e-6
SCALE = 128.0 ** -0.5
BIG = 1.0e5
NS = 4
SLOT_ELEMS = 4096
KB = 1024

_off = {}
_n = 0
for _name, _cnt in (("w_in", 16), ("w_gate", 16), ("w_attn", 8), ("w_out", 8), ("w_four", 4),
                    ("w_up", 44), ("w_down", 24), ("w_modf", 32), ("w_modg", 16)):
    _off[_name] = _n
    _n += _cnt
NW = _n
NDFT = 24


class Plan:
    def __init__(self):
        self.ops = {e: [] for e in ("pe", "act", "dve", "pool", "sp")}
        self.cnt = {e: 0 for e in ("pe", "act", "dve")}
        self.dcnt = {}
        self.waited = {}
        self.semh = {}
        self.pending = {}
        self.stopped = False

    def _filter(self, eng, waits):
        ws = []
        for w in waits:
            if w is None:
                continue
            s, v = w
            if v <= 0:
                continue
            if self.waited.get((eng, s), 0) >= v:
                continue
            self.waited[(eng, s)] = v
            ws.append((s, v))
        return ws

    def op(self, eng, meth, kw, waits=(), inc=True):
        if self.stopped:
            return (eng, self.cnt[eng])
        waits = list(waits)
        if eng in self.pending:
            waits += self.pending.pop(eng)
        ws = self._filter(eng, waits)
        if inc:
            self.cnt[eng] += 1
        c = self.cnt[eng]
        semh = self.semh

        def th(e):
            for s, v in ws:
                e.wait_ge(semh[s], v)
            ins = getattr(e, meth)(**kw)
            if inc:
                ins.then_inc(semh[eng], 1)
        self.ops[eng].append(th)
        return (eng, c)

    def dma(self, queue, out, in_, sem, waits=()):
        if self.stopped:
            return (sem, self.dcnt.get(sem, 0))
        ws = self._filter(queue, list(waits))
        self.dcnt[sem] = self.dcnt.get(sem, 0) + 16
        v = self.dcnt[sem]
        semh = self.semh

        def th(e):
            for s, vv in ws:
                e.wait_ge(semh[s], vv)
            e.dma_start(out=out, in_=in_).then_inc(semh[sem], 16)
        self.ops[queue].append(th)
        return (sem, v)

    def now(self, eng):
        return (eng, self.cnt[eng])

    def barrier(self):
        cur = [self.now(e) for e in ("pe", "act", "dve")]
        for e in ("pe", "act", "dve"):
            self.pending[e] = self.pending.get(e, []) + [w for w in cur if w[0] != e]


def build_program(stop=None):
    nc = bass.Bass("TRN2", target_bir_lowering=False)
    dumps = []

    def chk(name, views):
        if stop == name and not P.stopped:
            for nm, v, dt_ in views:
                dumps.append((nm, v, dt_, [P.now("pe"), P.now("act"), P.now("dve")]))
            P.stopped = True
    xs = nc.dram_tensor("xs", [3, S, D], F32, kind="ExternalInput").ap()
    c3t = nc.dram_tensor("c3t", [128, 16, 3], F32, kind="ExternalInput").ap()
    wts = nc.dram_tensor("wts", [NW, 128, SLOT_ELEMS], F32, kind="ExternalInput").ap()
    dft = nc.dram_tensor("dft", [NDFT, 128, SLOT_ELEMS], BF, kind="ExternalInput").ap()
    ident_d = nc.dram_tensor("ident", [128, 128], BF, kind="ExternalInput").ap()
    btab_d = nc.dram_tensor("btab", [128, 5, 512], BF, kind="ExternalInput").ap()
    identg_d = nc.dram_tensor("identg", [128, 4, 128], BF, kind="ExternalInput").ap()
    cc_d = nc.dram_tensor("cctab", [128, 2, 2, 256], BF, kind="ExternalInput").ap()
    gvec_d = nc.dram_tensor("gvec", [128, 2, 16], F32, kind="ExternalInput").ap()
    ccpos_d = nc.dram_tensor("ccpos", [128, 2, 256], BF, kind="ExternalInput").ap()
    altc_d = nc.dram_tensor("altc", [128, 2], BF, kind="ExternalInput").ap()
    bmodT_d = nc.dram_tensor("bmodT", [128, 64], F32, kind="ExternalInput").ap()
    bmodg_d = nc.dram_tensor("bmodg", [1, 4096], F32, kind="ExternalInput").ap()
    bgate_d = nc.dram_tensor("bgate", [128, 32], F32, kind="ExternalInput").ap()
    sink_d = nc.dram_tensor("sink", [1, 16], F32, kind="ExternalInput").ap()
    gfin_d = nc.dram_tensor("gfin", [1, D], F32, kind="ExternalInput").ap()
    y = nc.dram_tensor("y", [5120, D], F32, kind="ExternalOutput").ap()
    gates_d = nc.dram_tensor("gates_scr", [3, 4096], F32, kind="Internal").ap()

    P = Plan()
    OP = P.op
    ARENA_BYTES = 207 * KB
    stack = ExitStack()
    arena = stack.enter_context(nc.sbuf_tensor("arena", [128, ARENA_BYTES // 2], BF))
    ps = stack.enter_context(nc.psum_tensor("psum", [128, 4096], F32))

    def V(off, dtype, shape):
        esz = 4 if dtype == F32 else 2
        n = 1
        for s_ in shape[1:]:
            n *= s_
        nb = n * esz
        assert off % 4 == 0 and off + nb <= ARENA_BYTES, (off, nb)
        v = arena[:, off // 2: (off + nb) // 2]
        if dtype == F32:
            v = v.bitcast(F32)
        if len(shape) == 3:
            v = v.rearrange("p (a b) -> p a b", a=shape[1])
        elif len(shape) == 4:
            v = v.rearrange("p (a b c) -> p a b c", a=shape[1], b=shape[2])
        return v

    o = 0
    ringv = []
    for i in range(NS):
        ringv.append(V(o, BF, [128, SLOT_ELEMS]))
        o += 8 * KB
    ident = V(o, BF, [128, 128]); o += 256
    ones = V(o, BF, [128, 128]); o += 256
    cctab = V(o, BF, [128, 2, 2, 256]); o += 2048
    esink = V(o, F32, [128, 16]); o += 64
    bgate = V(o, F32, [128, 32]); o += 128
    gvec = V(o, F32, [128, 2, 16]); o += 128
    bmodT = V(o, F32, [128, 64]); o += 256
    modsb = V(o, F32, [128, 64, 3]); o += 768
    gmod = V(o, F32, [128, 2, 3, 16]); o += 384
    stat = V(o, F32, [128, 16]); o += 64
    scT = V(o, BF, [128, 16, 3]); o += 128
    gfin = V(o, F32, [128, D]); o += 8 * KB
    gates = V(o, F32, [128, 2, D]); o += 16 * KB
    kv_off = o
    kT = V(o, BF, [128, 4, S]); o += 16 * KB
    vv = V(o, BF, [128, 16, 512]); o += 16 * KB
    YT = V(o, BF, [128, 8, S]); o += 32 * KB
    hT_off = o
    hT = V(o, BF, [128, 16, 512]); o += 16 * KB
    xnew_off = o
    xnew = V(o, F32, [128, 4, D]); o += 32 * KB
    A0 = o
    assert ARENA_BYTES - A0 >= 32 * KB, (ARENA_BYTES - A0)
    uu = V(xnew_off, BF, [128, 16, 1024])
    atbt = V(hT_off, BF, [128, 16, 512])
    attnT = V(xnew_off, BF, [128, 16, 512])
    xst = [V(A0, F32, [128, D]), V(A0 + 8 * KB, F32, [128, D])]
    xn = [V(A0 + 16 * KB, BF, [128, D]), V(A0 + 20 * KB, BF, [128, D])]
    junk = V(A0 + 24 * KB, BF, [128, D])
    qT = [V(A0, BF, [128, 4, 512]), V(A0 + 4 * KB, BF, [128, 4, 512])]
    pT = [V(A0 + 8 * KB, BF, [128, 3, 512]), V(A0 + 11 * KB, BF, [128, 3, 512])]
    den = [V(A0 + 14 * KB, F32, [128, 512]), V(A0 + 16 * KB, F32, [128, 512])]
    rden = [V(A0 + 18 * KB, F32, [128, 512]), V(A0 + 20 * KB, F32, [128, 512])]
    btab = V(A0 + 26 * KB, BF, [128, 5, 512])
    identg = V(A0 + 31 * KB, BF, [128, 4, 128])
    mergedT = V(A0, BF, [128, 16, 512])
    gA = V(A0 + 16 * KB, F32, [128, 4, 512])
    gF = V(A0 + 24 * KB, F32, [128, 4, 512])
    tmpb = [V(A0 + 16 * KB + 2 * KB * q, F32, [128, 512]) for q in range(4)]
    xn2 = [V(A0 + 24 * KB, BF, [128, D]), V(A0 + 28 * KB, BF, [128, D])]
    actT = V(A0, BF, [128, 24, 512])
    sgx = [V(A0 + 24 * KB + 2 * KB * q, F32, [128, 512]) for q in range(4)]
    junk3 = V(A0 + 28 * KB, BF, [128, D])
    dbuf = V(A0, BF, [128, 16, 514])
    ccpos = V(A0 + 17 * KB, BF, [128, 2, 256])
    altc = V(A0 + 18 * KB, BF, [128, 2])
    col512 = V(A0 + 18 * KB + 64, BF, [128, 16])
    scbc = V(A0, BF, [128, 16, 3, 128])
    bmodg = V(A0 + 12 * KB, F32, [128, 4096])
    sinkb = V(A0 + 28 * KB, F32, [128, 16])
    c3sb = V(A0 + 29 * KB, F32, [128, 16, 3])
    gst = V(kv_off, F32, [128, 3 * 4096])

    def bank(b):
        return ps[:, b * 512:(b + 1) * 512]

    def bank4(b):
        return bank(b).rearrange("p (a b) -> p a b", a=4)

    def bankT(b0):
        return ps[:, b0 * 512:(b0 + 2) * 512].bitcast(BF).rearrange("p (a b) -> p a b", a=16)

    bank_free = {b: [] for b in range(8)}
    ring_entries = []

    def ring_acquire(src):
        i = len(ring_entries)
        slot = i % NS
        ring_entries.append([src, slot, None, P.stopped])
        return i, slot, ("ring%d" % slot, 16 * (i // NS + 1))

    def ring_release(i, w):
        ring_entries[i][2] = w

    def wsrc(name, idx):
        return wts[_off[name] + idx]

    def mm_group(srcs, nk, mode, act_fn, banks, evac, in_waits=(), c4list=(0, 1, 2, 3)):
        ready = {}
        first = True
        for si, src in enumerate(srcs):
            i, slot, w = ring_acquire(src)
            nkk = min(8, nk - si * 8)
            rv = ringv[slot].rearrange("p (k n) -> p k n", k=8)
            c = None
            for ci4, c4 in enumerate(c4list):
                for kk in range(nkk):
                    k = si * 8 + kk
                    if mode == "W":
                        lhsT = rv[:, kk, c4 * 128:(c4 + 1) * 128]
                        rhs = act_fn(k)
                    else:
                        lhsT = act_fn(k, c4)
                        rhs = rv[:, kk, :]
                    waits = []
                    if ci4 == 0 and kk == 0:
                        waits.append(w)
                    if first:
                        waits += list(in_waits)
                        first = False
                    if k == 0:
                        waits += bank_free[banks[ci4]]
                        bank_free[banks[ci4]] = []
                    last_slot = (ci4 == len(c4list) - 1 and kk == nkk - 1)
                    last_bank = (k == nk - 1)
                    c = OP("pe", "matmul", dict(out=bank(banks[ci4]), lhsT=lhsT, rhs=rhs, start=(k == 0),
                                                 stop=(k == nk - 1)), waits, inc=(last_slot or last_bank))
                    if last_bank:
                        ready[c4] = c
            ring_release(i, c)
        for ci4, c4 in enumerate(c4list):
            bank_free[banks[ci4]] = list(evac(c4, bank(banks[ci4]), ready[c4]))

    xbuf_free = {0: [], 1: []}
    import collections as _c
    xn_free = _c.defaultdict(list)
    tog = {"n": 0}

    pref = {}

    def x_prefetch(key, srows, extra_waits):
        lds = {}
        for j in range(2):
            lds[j] = P.dma("sp", xst[j], srows(j), "xld%d" % j, waits=xbuf_free[j] + list(extra_waits))
            xbuf_free[j] = []
        pref["key"] = key
        pref["lds"] = lds

    def norm_sub(j, xin, lw, gm, sh, dstT, xnb, dst_waits=(), on_stats=None):
        col = stat[:, j:j + 1]
        a1 = OP("act", "activation", dict(out=xnb, in_=xin, func=AF.Square, accum_out=col),
                list(lw) + list(dst_waits) + xn_free[id(xnb)])
        d1 = OP("act", "activation", dict(out=col, in_=col, func=AF.Sqrt, scale=1.0 / D, bias=EPS), [a1])
        d2 = OP("dve", "reciprocal", dict(out=col, in_=col), [d1])
        d3 = OP("dve", "tensor_scalar", dict(out=xnb[:, 0:1024], in0=xin[:, 0:1024], scalar1=col,
                                              scalar2=None, op0=ALU.mult), [d2] + list(lw))
        a3 = OP("act", "activation", dict(out=xnb[:, 1024:2048], in_=xin[:, 1024:2048], func=AF.Identity,
                                          scale=col), [d2])
        if on_stats is not None:
            on_stats([d3, a3])

        def emit():
            tb = 2 * (tog["n"] % 2)
            tog["n"] += 1
            pt = bankT(tb)
            w0 = [d3, a3] + bank_free[tb] + bank_free[tb + 1]
            bank_free[tb] = []
            bank_free[tb + 1] = []
            t = None
            for f in range(16):
                t = OP("pe", "transpose", dict(out=pt[:, f, :], in_=xnb[:, f * 128:(f + 1) * 128],
                                               identity=ident), w0 if f == 0 else (), inc=(f == 15))
            xn_free[id(xnb)] = [t]
            ev = None
            for f in range(16):
                dst = dstT[:, f, j * 128:(j + 1) * 128]
                ev = OP("dve", "tensor_scalar", dict(out=dst, in0=pt[:, f, :], scalar1=gm(f), scalar2=sh(f),
                                                      op0=ALU.mult, op1=ALU.add), [t] + list(dst_waits))
            bank_free[tb] = [ev]
            bank_free[tb + 1] = [ev]
        return emit

    def norm_stage(srows, src_sb, gm, sh, dstT, xbuf, xnbuf, jbuf, dst_waits=(), key=None):
        prev_em = None
        for j in range(4):
            b = j % 2
            if srows is not None:
                if key is not None and pref.get("key") == key and j in pref["lds"]:
                    ld = pref["lds"].pop(j)
                else:
                    ld = P.dma("sp", xbuf[b], srows(j), "xld%d" % b, waits=xbuf_free[b])
                xin = xbuf[b]
                lw = [ld]

                def rel(ws, b=b):
                    xbuf_free[b] = list(ws)
                em = norm_sub(j, xin, lw, gm, sh, dstT, xnbuf[b], dst_waits, on_stats=rel)
            else:
                em = norm_sub(j, src_sb(j), [], gm, sh, dstT, xnbuf[b], dst_waits)
            if prev_em is not None:
                prev_em()
            prev_em = em
        prev_em()

    def copy_evac(dst_fn, flip=0, guard=False, dve_only=False):
        def ev(c4, bk, rdy):
            dst = dst_fn(c4)
            gw = list(xguard) if guard else []
            if (c4 + flip) % 2 == 0 and not dve_only:
                return [OP("act", "activation", dict(out=dst, in_=bk, func=AF.Copy), [rdy] + gw)]
            return [OP("dve", "tensor_copy", dict(out=dst, in_=bk), [rdy] + gw)]
        return ev

    cl = None
    for dst, src in ((ident, ident_d), (cctab, cc_d), (gvec, gvec_d), (bmodT, bmodT_d),
                     (bgate, bgate_d), (gfin, gfin_d[0:1, :].partition_broadcast(128)),
                     (bmodg, bmodg_d[0:1, :].partition_broadcast(128)),
                     (sinkb, sink_d[0:1, :].partition_broadcast(128)), (c3sb, c3t)):
        cl = P.dma("sp", dst, src, "cld")
    cw = [cl]
    a_sc = OP("act", "activation", dict(out=scT, in_=c3sb, func=AF.Silu), cw)
    OP("act", "activation", dict(out=esink, in_=sinkb, func=AF.Exp), cw)
    OP("dve", "memset", dict(ap=ones, constant=1.0), cw)
    dsc = None
    for s_ in range(3):
        dsc = OP("dve", "tensor_copy", dict(out=scbc[:, :, s_, :],
                                             in_=scT[:, :, s_:s_ + 1].to_broadcast([128, 16, 128])), [a_sc])
    modps = bank(0).rearrange("p (a b) -> p a b", a=128)

    def modf_part(which):
        pe_last = None
        for si in range(16 * which, 16 * which + 16):
            i, slot, w = ring_acquire(wsrc("w_modf", si))
            rv = ringv[slot].rearrange("p (k n) -> p k n", k=16)
            c = None
            for fl in range(2):
                f = si * 2 + fl
                for k in range(16):
                    waits = []
                    if fl == 0 and k == 0:
                        waits = [w, a_sc]
                        if si == 16 * which:
                            waits += bank_free[0]
                            bank_free[0] = []
                    c = OP("pe", "matmul", dict(out=modps[:, f, 0:3], lhsT=rv[:, k, fl * 128:(fl + 1) * 128],
                                                 rhs=scT[:, k, :], start=(k == 0), stop=(k == 15)),
                           waits, inc=(fl == 1 and k == 15))
            ring_release(i, c)
            pe_last = c
        f0 = 32 * which
        dm = None
        for s_ in range(3):
            dm = OP("dve", "tensor_tensor", dict(out=modsb[:, f0:f0 + 32, s_], in0=modps[:, f0:f0 + 32, s_],
                                                 in1=bmodT[:, f0:f0 + 32], op=ALU.add), [pe_last] + cw)
        for s_ in range(3):
            dm = OP("dve", "scalar_tensor_tensor", dict(out=gmod[:, which, s_, :],
                                                        in0=modsb[:, f0 + 16:f0 + 32, s_], scalar=1.0,
                                                        in1=gvec[:, which, :], op0=ALU.add, op1=ALU.mult), [dm])
        bank_free[0] = [dm]
        return dm

    mod_ready = modf_part(0)
    mod2 = {}
    gl = None
    gbanks = [4, 5, 6]
    for cb in range(8):
        rdy = [None] * 3
        for si in range(2):
            i, slot, w = ring_acquire(wsrc("w_modg", cb * 2 + si))
            rv = ringv[slot].rearrange("p (k n) -> p k n", k=8)
            c = None
            for s_ in range(3):
                for kk in range(8):
                    k = si * 8 + kk
                    waits = []
                    if s_ == 0 and kk == 0:
                        waits += [w, dsc]
                    if k == 0:
                        waits += bank_free[gbanks[s_]]
                        bank_free[gbanks[s_]] = []
                    c = OP("pe", "matmul", dict(out=bank(gbanks[s_]), lhsT=scbc[:, k, s_, :], rhs=rv[:, kk, :],
                                                 start=(k == 0), stop=(k == 15)), waits,
                           inc=((s_ == 2 and kk == 7) or k == 15))
                    if k == 15:
                        rdy[s_] = c
            ring_release(i, c)
        for s_ in range(3):
            gl = OP("dve", "tensor_tensor", dict(out=gst[0:1, s_ * 4096 + cb * 512: s_ * 4096 + (cb + 1) * 512],
                                                 in0=bank(gbanks[s_])[0:1, :],
                                                 in1=bmodg[0:1, cb * 512:(cb + 1) * 512], op=ALU.add),
                    [rdy[s_]] + cw)
            bank_free[gbanks[s_]] = [gl]
    gstore = P.dma("sp", gates_d.rearrange("(o s) n -> o (s n)", o=1), gst[0:1, :], "gst", waits=[gl])
    chk("prologue", [("modsb", modsb, F32), ("gmod", gmod, F32), ("gst", gst[0:1, :], F32), ("esink", esink, F32)])

    sb_free = {0: [], 1: []}
    pT_free = {0: [], 1: []}
    den_free = {0: [], 1: []}
    qfree = {0: [], 1: []}
    g_free = {"A": [], "F": []}
    sg_free = {0: [], 1: [], 2: [], 3: []}
    yst_free = {0: [], 1: []}
    SUMB = 7
    xguard = []
    ystores_last = []
    tctr = [0]
    tmp_free = {0: [], 1: [], 2: [], 3: []}

    def take_guard():
        g_ = list(xguard)
        del xguard[:]
        return g_

    for sl in range(3):
        n_own = 4 if sl < 2 else 2
        half = (sl == 2)
        yrow0 = sl * 2048
        P.barrier()
        gld = P.dma("sp", gates.rearrange("p a b -> p (a b)"), gates_d[sl:sl + 1, :].partition_broadcast(128),
                    "gld", waits=[gstore, P.now("dve"), P.now("act"), P.now("pe")])

        def gm1(f, sl=sl):
            return gmod[:, 0, sl, f:f + 1]

        def sh1(f, sl=sl):
            return modsb[:, f, sl:sl + 1]

        def gm2(f, sl=sl):
            return gmod[:, 1, sl, f:f + 1]

        def sh2(f, sl=sl):
            return modsb[:, 32 + f, sl:sl + 1]

        def xrows(t):
            return lambda j: xs[sl, t * 512 + j * 128: t * 512 + (j + 1) * 128, :]

        for t in range(4):
            P.barrier()
            norm_stage(xrows(t), None, gm1, sh1, hT, xst, xn, junk, dst_waits=[mod_ready], key=("pro", sl, t))
            hready = [P.now("dve"), P.now("act")]
            chk("n0", [("hT", hT, BF), ("xn", xn[1], BF), ("stat", stat, F32)])
            mm_group([wsrc("w_in", 8), wsrc("w_in", 9)], 16, "W", lambda k: hT[:, k, :], [4, 5, 6, 7],
                     copy_evac(lambda c4: kT[:, c4, t * 512:(t + 1) * 512]), in_waits=hready)
            chk("k0", [("kT", kT, BF)])
            mm_group([wsrc("w_in", 10), wsrc("w_in", 11)], 16, "A", lambda k, j: hT[:, k, j * 128:(j + 1) * 128],
                     [0, 1, 2, 3], copy_evac(lambda c4: vv[:, t * 4 + c4, :], 1))
            for uh in range(2):
                mm_group([wsrc("w_in", 12 + uh * 2), wsrc("w_in", 13 + uh * 2)], 16, "A",
                         lambda k, j: hT[:, k, j * 128:(j + 1) * 128], [4, 5, 6, 7] if uh == 0 else [0, 1, 2, 3],
                         copy_evac(lambda c4: uu[:, t * 4 + c4, uh * 512:(uh + 1) * 512], uh, guard=True))
            if sl == 0 and t == 0:
                mod2["r"] = modf_part(1)
        P.barrier()
        chk("kvu", [("kT", kT, BF), ("vv", vv, BF), ("uu", uu, BF), ("hT", hT, BF)])
        def pos_dft(base, dst):
            for tb_ in range(2):
                e0 = ring_acquire(dft[base + tb_ * 2])
                e1 = ring_acquire(dft[base + tb_ * 2 + 1])
                rvs = [ringv[e0[1]].rearrange("p (k n) -> p k n", k=8),
                       ringv[e1[1]].rearrange("p (k n) -> p k n", k=8)]
                c = None
                for cc in range(8):
                    bk = cc
                    for sc in range(16):
                        waits = []
                        if cc == 0 and sc == 0:
                            waits.append(e0[2])
                        if cc == 0 and sc == 8:
                            waits.append(e1[2])
                        if sc == 0:
                            waits += bank_free[bk]
                            bank_free[bk] = []
                        c = OP("pe", "matmul", dict(out=bank(bk), lhsT=uu[:, sc, cc * 128:(cc + 1) * 128],
                                                     rhs=rvs[sc // 8][:, sc % 8, :], start=(sc == 0), stop=(sc == 15)),
                               waits, inc=(sc == 15 or (cc == 7 and sc == 7)))
                        if cc == 7 and sc == 7:
                            ring_release(e0[0], c)
                    d_ = dst(tb_, cc)
                    if cc % 2 == 0:
                        r = OP("act", "activation", dict(out=d_, in_=bank(bk), func=AF.Copy), [c])
                    else:
                        r = OP("dve", "tensor_copy", dict(out=d_, in_=bank(bk)), [c])
                    bank_free[bk] = [r]
                ring_release(e1[0], c)

        def chan_dft(rhs, mirror, kt):
            evw = [P.now("act"), P.now("dve")]
            for g in range(4):
                for mc in range(2):
                    oc = g * 2 + mc
                    bk = oc
                    n = 0
                    c = None
                    for tb_ in range(2):
                        for ci in range(2):
                            waits = []
                            if n == 0:
                                waits = list(evw) + bank_free[bk]
                                bank_free[bk] = []
                            if mirror and tb_ == 1:
                                lt = ccpos[:, ci, mc * 128:(mc + 1) * 128]
                            else:
                                lt = cctab[:, tb_, ci, mc * 128:(mc + 1) * 128]
                            c = OP("pe", "matmul", dict(out=bank(bk), lhsT=lt, rhs=rhs(tb_ * 8 + g * 2 + ci),
                                                         start=(n == 0), stop=(n == 3)), waits, inc=(n == 3))
                            n += 1
                    d_ = YT[:, oc, kt * 512:(kt + 1) * 512]
                    if oc % 2 == 0:
                        r = OP("act", "activation", dict(out=d_, in_=bank(bk), func=AF.Copy), [c])
                    else:
                        r = OP("dve", "tensor_copy", dict(out=d_, in_=bank(bk)), [c])
                    bank_free[bk] = [r]

        if half:
            for kt in range(n_own):
                pos_dft(16 + kt * 4, lambda tb_, cc: atbt[:, tb_ * 8 + cc, :])
                chan_dft(lambda ch: atbt[:, ch, :], False, kt)
                P.barrier()
        else:
            sw = [P.now("act"), P.now("dve"), P.now("pe")]
            P.dma("sp", ccpos, ccpos_d, "cld", waits=sw)
            cld2 = P.dma("sp", altc, altc_d, "cld", waits=sw)
            for p_ in (1, 0):
                pos_dft(p_ * 4, lambda tb_, cc: dbuf[:, tb_ * 8 + cc, 0:512])
                if p_ == 1:
                    c = None
                    for cc in range(8):
                        for sc in range(16):
                            waits = []
                            if cc == 0 and sc == 0:
                                waits = [cld2] + bank_free[7]
                                bank_free[7] = []
                            c = OP("pe", "matmul", dict(out=bank(7)[:, cc:cc + 1], lhsT=uu[:, sc, cc * 128:(cc + 1) * 128],
                                                         rhs=altc[:, 0:1], start=(sc == 0), stop=(sc == 15)),
                                   waits, inc=(cc == 7 and sc == 15))
                    aw = [P.now("act")]
                    r1 = OP("dve", "tensor_copy", dict(out=dbuf[:, 0:8, 512], in_=bank(7)[:, 0:8]), [c] + aw)
                    OP("dve", "memset", dict(ap=dbuf[:, 8:16, 512], constant=0.0), aw)
                    OP("dve", "tensor_copy", dict(out=col512, in_=dbuf[:, :, 0]), aw + [P.now("dve")])
                    bank_free[7] = [r1]
                else:
                    OP("dve", "tensor_copy", dict(out=dbuf[:, :, 512], in_=col512), [P.now("act"), P.now("dve")])
                chan_dft(lambda ch: dbuf[:, ch, 0:512], False, p_)
                chan_dft(lambda ch: dbuf[:, ch, 512:0:-1], True, 3 - p_)
                P.barrier()
            for b_ in (0, 1):
                xbuf_free[b_] = xbuf_free[b_] + [P.now("pe"), P.now("dve"), P.now("act")]

        chk("dft", [("YT", YT, BF)])
        for t in range(n_own):
            P.barrier()
            norm_stage(xrows(t), None, gm1, sh1, hT, xst, xn, junk, key=("tile", sl, t))
            P.barrier()
            chk("s0", [("hT", hT, BF)])
            bl = P.dma("sp", btab, btab_d, "cld", waits=[P.now("act"), P.now("dve"), P.now("pe")])
            bl = P.dma("sp", identg, identg_d, "cld", waits=[P.now("act"), P.now("dve"), P.now("pe")])
            unit = 0
            pend = []

            def flush_pv():
                while pend:
                    (g_, qb_i, rs_, ub_, exps_, pTu_, denu_) = pend.pop(0)
                    rdenu_ = rden[ub_]
                    OB = 3 + ub_
                    SB_ = 5 + ub_
                    nr = len(rs_)
                    c = None
                    for idx, (r, kb, di) in enumerate(rs_):
                        waits = [exps_[idx]]
                        if idx == 0:
                            waits += bank_free[SB_]
                            bank_free[SB_] = []
                        c = OP("pe", "matmul", dict(out=bank(SB_), lhsT=ones, rhs=pTu_[:, r, :], start=(idx == 0),
                                                     stop=(idx == nr - 1)), waits, inc=(idx == nr - 1))
                    c_sum = c
                    for idx, (r, kb, di) in enumerate(rs_):
                        waits = []
                        if idx == 0:
                            waits += bank_free[OB]
                            bank_free[OB] = []
                        c = OP("pe", "matmul", dict(out=bank(OB), lhsT=vv[:, kb, g_ * 128:(g_ + 1) * 128],
                                                     rhs=pTu_[:, r, :], start=(idx == 0), stop=(idx == nr - 1)),
                               waits, inc=(idx == nr - 1))
                    c_out = c
                    pT_free[ub_] = [c_out]
                    d1 = OP("dve", "tensor_tensor",
                            dict(out=denu_.rearrange("p (a b) -> p a b", a=4), in0=bank4(SB_),
                                 in1=esink[:, 4 * g_:4 * g_ + 4].unsqueeze(2).to_broadcast([128, 4, 128]),
                                 op=ALU.add), [c_sum] + den_free[ub_])
                    l1 = OP("act", "activation", dict(out=rdenu_, in_=denu_, func=AF.Ln), [d1])
                    d2 = OP("act", "activation", dict(out=rdenu_, in_=rdenu_, func=AF.Exp, scale=-1.0), [l1])
                    d3 = OP("dve", "tensor_tensor",
                            dict(out=attnT[:, 4 * g_:4 * g_ + 4, qb_i * 128:(qb_i + 1) * 128], in0=bank4(OB),
                                 in1=rdenu_.rearrange("p (a b) -> p a b", a=4), op=ALU.mult),
                            [d2, c_out] + take_guard())
                    bank_free[OB] = [d3]
                    bank_free[SB_] = [d1]
                    den_free[ub_] = [d3]

            for g in range(4):
                qb_ = qT[g % 2]
                mm_group([wsrc("w_in", g * 2), wsrc("w_in", g * 2 + 1)], 16, "W", lambda k: hT[:, k, :],
                         [4, 5, 6, 7], copy_evac(lambda c4: qb_[:, c4, :], dve_only=True), in_waits=qfree[g % 2])
                qready = [P.now("act"), P.now("dve")]
                for qb in range(4):
                    i_loc = t * 4 + qb
                    rs = []
                    for r in range(3):
                        kbi = i_loc + r - 1
                        if half:
                            di = r
                            if r == 0 and i_loc == 0:
                                di = 3
                            if r == 2 and i_loc == 7:
                                di = 4
                            rs.append((r, kbi % 16, di))
                        elif 0 <= kbi < 16:
                            rs.append((r, kbi, r))
                    ub = unit % 2
                    sbanks = [0, 1, 2]
                    unit += 1
                    pTu, denu = pT[ub], den[ub]
                    c = None
                    for (r, kb, di) in rs:
                        bk = sbanks[r]
                        waits = list(qready) + bank_free[bk] + [bl]
                        bank_free[bk] = []
                        OP("pe", "matmul", dict(out=bank4(bk), lhsT=kT[:, g, kb * 128:(kb + 1) * 128],
                                                 rhs=qb_[:, :, qb * 128:(qb + 1) * 128], start=True, stop=False),
                           waits, inc=False)
                        c = OP("pe", "matmul", dict(out=bank(bk), lhsT=identg[:, g, :], rhs=btab[:, di, :],
                                                     start=False, stop=True), inc=(r == rs[-1][0]))
                    r0_, r1_ = rs[0][0], rs[-1][0] + 1
                    ex = OP("act", "activation",
                            dict(out=pTu[:, r0_:r1_, :],
                                 in_=ps[:, r0_ * 512:r1_ * 512].rearrange("p (a b) -> p a b", a=r1_ - r0_),
                                 func=AF.Exp, scale=SCALE), [c] + pT_free[ub])
                    exps = [ex] * len(rs)
                    for (r, kb, di) in rs:
                        bank_free[sbanks[r]] = [ex]
                    if qb == 3:
                        qfree[g % 2] = [c]
                    flush_pv()
                    pend.append((g, qb, rs, ub, exps, pTu, denu))
            flush_pv()
            P.barrier()
            chk("s1", [("attnT", attnT, BF)])
            xl = {}
            for j in (2, 3):
                xl[j] = P.dma("sp", xnew[:, j, :], xs[sl, t * 512 + j * 128: t * 512 + (j + 1) * 128, :],
                              "x2ld%d" % (j % 2), waits=[P.now("dve"), P.now("act"), P.now("pe")] + ystores_last)
            for cg in range(4):
                sigA, mulA, sigF, fin = [], [], [], []

                def sigA_evac(c4, bk, rdy):
                    r = OP("act", "activation", dict(out=gA[:, c4, :], in_=bk, func=AF.Sigmoid,
                                                     bias=bgate[:, cg * 4 + c4: cg * 4 + c4 + 1]),
                           [rdy] + g_free["A"])
                    sigA.append(r)
                    return [r]

                def mulA_evac(c4, bk, rdy):
                    r = OP("dve", "tensor_tensor", dict(out=gA[:, c4, :], in0=bk, in1=gA[:, c4, :], op=ALU.mult),
                           [rdy, sigA[c4]])
                    mulA.append(r)
                    return [r]

                def sigF_evac(c4, bk, rdy):
                    r = OP("act", "activation", dict(out=gF[:, c4, :], in_=bk, func=AF.Sigmoid,
                                                     bias=bgate[:, 16 + cg * 4 + c4: 16 + cg * 4 + c4 + 1]),
                           [rdy] + g_free["F"])
                    sigF.append(r)
                    return [r]

                def mulF_evac(c4, bk, rdy):
                    r1 = OP("dve", "tensor_tensor", dict(out=gF[:, c4, :], in0=bk, in1=gF[:, c4, :], op=ALU.mult),
                            [rdy, sigF[c4]])
                    r2 = OP("dve", "tensor_tensor", dict(out=mergedT[:, cg * 4 + c4, :], in0=gA[:, c4, :],
                                                          in1=gF[:, c4, :], op=ALU.add), [r1, mulA[c4]])
                    fin.append(r2)
                    return [r1]
                mm_group([wsrc("w_gate", cg * 2), wsrc("w_gate", cg * 2 + 1)], 16, "W", lambda k: hT[:, k, :],
                         [0, 1, 2, 3], sigA_evac)
                mm_group([wsrc("w_attn", cg * 2), wsrc("w_attn", cg * 2 + 1)], 16, "W", lambda k: attnT[:, k, :],
                         [4, 5, 6, 7], mulA_evac)
                mm_group([wsrc("w_gate", (4 + cg) * 2), wsrc("w_gate", (4 + cg) * 2 + 1)], 16, "W",
                         lambda k: hT[:, k, :], [0, 1, 2, 3], sigF_evac)
                mm_group([wsrc("w_four", cg)], 8, "W", lambda k: YT[:, k, t * 512:(t + 1) * 512], [4, 5, 6, 7],
                         mulF_evac)
                g_free["A"] = [fin[-1]]
                g_free["F"] = [fin[-1]]
            P.barrier()
            chk("s3", [("mergedT", mergedT, BF)])
            for j in (0, 1):
                xl[j] = P.dma("sp", xnew[:, j, :], xs[sl, t * 512 + j * 128: t * 512 + (j + 1) * 128, :],
                              "x2ld%d" % (j % 2), waits=[P.now("dve"), P.now("act"), P.now("pe")] + ystores_last)
            emits = []
            ti = 0
            for ph, subs in enumerate(((2, 3), (0, 1))):
                for cb in range(4):
                    def res_evac(c4, bk, rdy):
                        nonlocal_ti = tctr[0]
                        tctr[0] += 1
                        tb_ = tmpb[nonlocal_ti % 4]
                        r1 = OP("dve", "tensor_tensor", dict(out=tb_, in0=bk, in1=gates[:, 0, cb * 512:(cb + 1) * 512],
                                                             op=ALU.mult), [rdy, gld] + tmp_free[nonlocal_ti % 4])
                        r2 = OP("dve", "tensor_tensor", dict(out=xnew[:, c4, cb * 512:(cb + 1) * 512],
                                                             in0=xnew[:, c4, cb * 512:(cb + 1) * 512], in1=tb_,
                                                             op=ALU.add), [r1, xl[c4]])
                        tmp_free[nonlocal_ti % 4] = [r2]
                        return [r1]
                    mm_group([wsrc("w_out", cb * 2), wsrc("w_out", cb * 2 + 1)], 16, "A",
                             lambda k, j: mergedT[:, k, j * 128:(j + 1) * 128],
                             [4, 5] if cb % 2 == 0 else [6, 7], res_evac, c4list=subs)
                    if ph == 1 and cb in (1, 2):
                        emits[cb - 1]()
                if ph == 0:
                    for qi, j in enumerate((2, 3)):
                        emits.append(norm_sub(j, xnew[:, j, :], [P.now("dve")], gm2, sh2, hT, xn2[qi],
                                              dst_waits=[mod2["r"]]))
            em0 = norm_sub(0, xnew[:, 0, :], [P.now("dve")], gm2, sh2, hT, xn2[0])
            em1 = norm_sub(1, xnew[:, 1, :], [P.now("dve")], gm2, sh2, hT, xn2[1])
            em0()
            em1()
            P.barrier()
            chk("s4", [("xnew", xnew, F32)])
            chk("s5", [("h2T", hT, BF)])
            act_free = []
            for hf in range(2):
                cbs = list(range(0, 6)) if hf == 0 else list(range(6, 11))
                nch = 24 if hf == 0 else 20
                for ci_, cb in enumerate(cbs):
                    sgr = [None] * 4

                    def silu_evac(c4, bk, rdy):
                        r = OP("act", "activation", dict(out=sgx[c4], in_=bk, func=AF.Silu), [rdy] + sg_free[c4])
                        sgr[c4] = r
                        return [r]

                    def up_evac(c4, bk, rdy):
                        r = OP("dve", "tensor_tensor", dict(out=actT[:, ci_ * 4 + c4, :], in0=bk, in1=sgx[c4],
                                                            op=ALU.mult), [rdy, sgr[c4]] + act_free)
                        sg_free[c4] = [r]
                        return [r]
                    mm_group([wsrc("w_up", cb * 2), wsrc("w_up", cb * 2 + 1)], 16, "W", lambda k: hT[:, k, :],
                             [0, 1, 2, 3], silu_evac)
                    mm_group([wsrc("w_up", (11 + cb) * 2), wsrc("w_up", (11 + cb) * 2 + 1)], 16, "W",
                             lambda k: hT[:, k, :], [4, 5, 6, 7], up_evac)
                aready = [P.now("dve")]
                for cb in range(4):
                    def down_evac(c4, bk, rdy):
                        r1 = OP("dve", "tensor_tensor", dict(out=sgx[c4], in0=bk,
                                                             in1=gates[:, 1, cb * 512:(cb + 1) * 512], op=ALU.mult),
                                [rdy] + sg_free[c4])
                        r2 = OP("dve", "tensor_tensor", dict(out=xnew[:, c4, cb * 512:(cb + 1) * 512],
                                                             in0=xnew[:, c4, cb * 512:(cb + 1) * 512], in1=sgx[c4],
                                                             op=ALU.add), [r1])
                        sg_free[c4] = [r2]
                        return [r1]
                    base = hf * 12 + cb * 3
                    mm_group([wsrc("w_down", base), wsrc("w_down", base + 1), wsrc("w_down", base + 2)], nch, "A",
                             lambda k, j: actT[:, k, j * 128:(j + 1) * 128],
                             [0, 1, 2, 3] if cb % 2 == 0 else [4, 5, 6, 7], down_evac, in_waits=aready)
                act_free = [P.now("pe")]
            P.barrier()
            chk("s6", [("xnew", xnew, F32)])
            nxt = None
            if t + 1 < n_own:
                nxt = (("tile", sl, t + 1), xrows(t + 1))
            elif sl + 1 < 3:
                nxt = (("pro", sl + 1, 0), (lambda sl2: (lambda j: xs[sl2, j * 128:(j + 1) * 128, :]))(sl + 1))
            if nxt is not None:
                x_prefetch(nxt[0], nxt[1], [P.now("pe"), P.now("dve"), P.now("act")])
            for j in range(4):
                col = stat[:, 8 + j: 9 + j]
                a1 = OP("act", "activation", dict(out=junk3, in_=xnew[:, j, :], func=AF.Square, accum_out=col))
                d1 = OP("act", "activation", dict(out=col, in_=col, func=AF.Sqrt, scale=1.0 / D, bias=EPS), [a1])
                d2 = OP("dve", "reciprocal", dict(out=col, in_=col), [d1])
                d3 = OP("dve", "scalar_tensor_tensor", dict(out=xnew[:, j, :], in0=xnew[:, j, :], scalar=col, in1=gfin,
                                                            op0=ALU.mult, op1=ALU.mult), [d2, a1])
                r0 = yrow0 + t * 512 + j * 128
                st = P.dma("sp", y[r0:r0 + 128, :], xnew[:, j, :], "yst%d" % (j % 2), waits=[d3])
                xguard.append(st)
                if j == 0:
                    del ystores_last[:]
                ystores_last.append(st)
    P.stopped = False
    for nm, v, dt_, ws in dumps:
        dd = nc.dram_tensor("dbg_" + nm, list(v.shape), dt_, kind="ExternalOutput").ap()
        P.dma("sp", dd, v, "yst0", waits=ws)
    fin_waits = [(k_, P.dcnt[k_]) for k_ in ("yst0", "yst1") if k_ in P.dcnt]

    def fin_th(e):
        for s_, v_ in fin_waits:
            e.wait_ge(P.semh[s_], v_)
    P.ops["sp"].append(fin_th)
    for i, (src, slot, rel, dead) in enumerate(ring_entries):
        if dead:
            break
        waits = [ring_entries[i - NS][2]] if i >= NS else []
        P.dma("pool", ringv[slot], src, "ring%d" % slot, waits)

    names = ["pe", "act", "dve"] + sorted(P.dcnt.keys())
    for nm in names:
        P.semh[nm] = stack.enter_context(nc.semaphore(nm))
    assert max(P.cnt.values()) < 60000 and max(P.dcnt.values()) < 60000, (P.cnt, P.dcnt)
    with nc.Block() as block:
        @block.tensor
        def _(e):
            for th in P.ops["pe"]:
                th(e)

        @block.scalar
        def _(e):
            for th in P.ops["act"]:
                th(e)

        @block.vector
        def _(e):
            for th in P.ops["dve"]:
                th(e)

        @block.gpsimd
        def _(e):
            for th in P.ops["pool"]:
                th(e)

        @block.sync
        def _(e):
            for th in P.ops["sp"]:
                th(e)
    stack.close()
    return nc


def _fmtA(W):
    K, N = W.shape
    KS = -(-K // 1024)
    if KS * 1024 != K:
        W = np.concatenate([W, np.zeros((KS * 1024 - K, N), W.dtype)], axis=0)
    NB = N // 512
    return np.ascontiguousarray(W.reshape(KS, 8, 128, NB, 512).transpose(3, 0, 2, 1, 4)).reshape(NB * KS, 128, 4096)


def _const_tables():
    bf = ml_dtypes.bfloat16
    s = np.arange(2048, dtype=np.int64)
    ident = np.eye(128, dtype=np.float32).astype(bf)
    m = np.arange(256, dtype=np.int64)
    ang = 2.0 * np.pi * ((m[:, None] * m[None, :]) % 256) / 256.0
    cc = np.stack([np.cos(ang) / 16.0, -np.sin(ang) / 16.0], 0)
    cctab = cc.reshape(2, 2, 128, 256).transpose(2, 0, 1, 3).astype(np.float32).astype(bf)
    ccpos = np.ascontiguousarray((np.sin(ang) / 16.0).reshape(2, 128, 256).transpose(1, 0, 2)).astype(np.float32).astype(bf)
    altc = np.stack([((-1.0) ** np.arange(128)) / np.sqrt(2048.0), np.zeros(128)], 1).astype(np.float32).astype(bf)

    def pos_tab(off):
        nk = 2048 if off is None else 1024
        o_ = 0 if off is None else off
        sg = (s + o_) % 2048
        kg = (np.arange(nk, dtype=np.int64) + o_) % 2048
        ang = 2.0 * np.pi * ((sg[:, None] * kg[None, :]) % 2048) / 2048.0
        out = []
        for kt in range(nk // 512):
            for fn in (np.cos, np.sin):
                T = (fn(ang[:, kt * 512:(kt + 1) * 512]) / np.sqrt(2048.0)).astype(np.float32)
                out.append(_fmtA(T))
        return np.concatenate(out, 0)
    full = pos_tab(None).astype(bf)
    halves = [pos_tab(0).astype(bf), pos_tab(1024).astype(bf)]
    a = np.arange(128)[None, :]
    j = np.arange(128)[:, None]
    prev = (a - j + 128).astype(np.float32)
    prev = np.where(prev <= 128, prev, BIG)
    cur = np.abs(a - j).astype(np.float32)
    nxt = (j + 128 - a).astype(np.float32)
    nxt = np.where(nxt <= 128, nxt, BIG)
    allbig = np.full((128, 128), BIG, np.float32)
    hs = (2.0 ** (-(np.arange(4) + 1) / 2.0)).astype(np.float64)

    def mk(tabs):
        t_ = np.stack(tabs, 1).astype(np.float64)
        b_ = -(t_[:, :, None, :] * hs[None, None, :, None]) / SCALE
        return b_.reshape(128, 5, 512).astype(np.float32).astype(bf)
    dist = [mk([prev, cur, nxt, allbig, nxt]), mk([prev, cur, nxt, prev, allbig])]
    identg = np.stack([np.eye(128) * (4.0 ** -g) for g in range(4)], 1).astype(np.float32).astype(bf)
    return ident, cctab, full, halves, dist, identg, ccpos, altc


def prepare(x_prompt, x_sample, c_prompt, c_sample, w_mod, b_mod, g_mix, w_in, attn_sink, w_attn_branch,
            w_fourier_branch, w_gate, b_gate, w_out, g_ffn, w_up, w_down, g_final):
    f32 = np.float32
    X = np.concatenate([np.asarray(x_prompt, f32), np.asarray(x_sample, f32)], 0)
    C = np.concatenate([np.asarray(c_prompt, f32), np.asarray(c_sample, f32)], 0)
    w_mod = np.asarray(w_mod, f32)[0]; b_mod = np.asarray(b_mod, f32)[0]
    w_in = np.asarray(w_in, f32)[0]; w_gate = np.asarray(w_gate, f32)[0]
    w_attn = np.asarray(w_attn_branch, f32)[0]; w_four = np.asarray(w_fourier_branch, f32)[0]
    w_out = np.asarray(w_out, f32)[0]; w_up = np.asarray(w_up, f32)[0]; w_down = np.asarray(w_down, f32)[0]
    g_mix = np.asarray(g_mix, f32)[0]; g_ffn = np.asarray(g_ffn, f32)[0]; g_final = np.asarray(g_final, f32)
    b_gate = np.asarray(b_gate, f32)[0]; sink = np.asarray(attn_sink, f32)[0]

    wmf = np.concatenate([w_mod[:, 0:2048], w_mod[:, 2048:4096], w_mod[:, 6144:8192], w_mod[:, 8192:10240]], 1)
    wmf_s = np.ascontiguousarray(wmf.reshape(16, 128, 32, 256).transpose(2, 1, 0, 3)).reshape(32, 128, 4096)
    wmg = np.concatenate([w_mod[:, 4096:6144], w_mod[:, 10240:12288]], 1)
    wd0 = _fmtA(w_down[0:3072])
    wd1 = _fmtA(w_down[3072:5632])
    wts = np.concatenate([_fmtA(w_in), _fmtA(w_gate), _fmtA(w_attn), _fmtA(w_out), _fmtA(w_four), _fmtA(w_up),
                          wd0, wd1, wmf_s, _fmtA(wmg)], 0)
    assert wts.shape[0] == NW, wts.shape
    bmf = np.concatenate([b_mod[0:2048], b_mod[2048:4096], b_mod[6144:8192], b_mod[8192:10240]])
    bmodT = np.ascontiguousarray(bmf.reshape(64, 128).T)
    bmodg = np.concatenate([b_mod[4096:6144], b_mod[10240:12288]])[None, :]
    bgate = np.ascontiguousarray(b_gate.reshape(32, 128).T)
    gvec = np.ascontiguousarray(np.stack([g_mix.reshape(16, 128).T, g_ffn.reshape(16, 128).T], 1))
    ident, cctab, dft_full, dft_halves, dists, identg, ccpos, altc = _const_tables()

    in_maps = []
    meta = []
    for i in range(NCORES):
        if i % 2 == 0:
            fa = (5 * i) // 2; fb = fa + 1; hc = fa + 2; off = 0
        else:
            hc = (5 * i) // 2; fa = hc + 1; fb = hc + 2; off = 1024
        meta.append((fa, fb, hc, off))
        xs = np.stack([X[fa], X[fb], np.roll(X[hc], -off, axis=0)], 0)
        c3 = np.stack([C[fa], C[fb], C[hc]], 0)
        c3t = np.ascontiguousarray(c3.reshape(3, 16, 128).transpose(2, 1, 0))
        dftc = np.concatenate([dft_full, dft_halves[i % 2]], 0)
        in_maps.append({
            "xs": np.ascontiguousarray(xs), "c3t": c3t, "wts": wts, "dft": dftc, "ident": ident,
            "btab": dists[i % 2], "identg": identg, "cctab": cctab, "ccpos": ccpos, "altc": altc, "gvec": gvec, "bmodT": bmodT, "bmodg": bmodg, "bgate": bgate,
            "sink": np.ascontiguousarray(sink[None, :]), "gfin": np.ascontiguousarray(g_final[None, :]),
        })
    return in_maps, meta


_PROGRAM = {}


def kernel(**inputs):
    in_maps, meta = prepare(**inputs)
    if "nc" not in _PROGRAM:
        _PROGRAM["nc"] = build_program()
    res = run_bass_kernel_spmd(_PROGRAM["nc"], in_maps, core_ids=list(range(NCORES)))
    Y = np.zeros((20, S, D), np.float32)
    for i in range(NCORES):
        fa, fb, hc, off = meta[i]
        yc = np.asarray(res.results[i]["y"], np.float32)
        Y[fa] = yc[0:2048]
        Y[fb] = yc[2048:4096]
        Y[hc, off:off + 1024] = yc[4096:5120]
    return Y[:4].copy(), Y[4:].copy()
```

```python
from contextlib import ExitStack
import numpy as np
import ml_dtypes
import concourse.bass as bass
import concourse.mybir as mybir
from concourse.bass_utils import run_bass_kernel_spmd

F32 = mybir.dt.float32
BF = mybir.dt.bfloat16
AF = mybir.ActivationFunctionType
ALU = mybir.AluOpType

D = 2048
S = 2048
NH = 16
DFF = 5632
NCORES = 8
EPS = 1e-6
SCALE = 128.0 ** -0.5
BIG = 1.0e5
NS = 4
SLOT_ELEMS = 4096
KB = 1024

_off = {}
_n = 0
for _name, _cnt in (("w_in", 16), ("w_gate", 16), ("w_attn", 8), ("w_out", 8), ("w_four", 4),
                    ("w_up", 44), ("w_down", 24), ("w_modf", 32), ("w_modg", 16)):
    _off[_name] = _n
    _n += _cnt
NW = _n
NDFT = 24


class Plan:
    def __init__(self):
        self.ops = {e: [] for e in ("pe", "act", "dve", "pool", "sp")}
        self.cnt = {e: 0 for e in ("pe", "act", "dve")}
        self.dcnt = {}
        self.waited = {}
        self.semh = {}
        self.pending = {}
        self.stopped = False

    def _filter(self, eng, waits):
        ws = []
        for w in waits:
            if w is None:
                continue
            s, v = w
            if v <= 0:
                continue
            if self.waited.get((eng, s), 0) >= v:
                continue
            self.waited[(eng, s)] = v
            ws.append((s, v))
        return ws

    def op(self, eng, meth, kw, waits=(), inc=True):
        if self.stopped:
            return (eng, self.cnt[eng])
        waits = list(waits)
        if eng in self.pending:
            waits += self.pending.pop(eng)
        ws = self._filter(eng, waits)
        if inc:
            self.cnt[eng] += 1
        c = self.cnt[eng]
        semh = self.semh

        def th(e):
            for s, v in ws:
                e.wait_ge(semh[s], v)
            ins = getattr(e, meth)(**kw)
            if inc:
                ins.then_inc(semh[eng], 1)
        self.ops[eng].append(th)
        return (eng, c)

    def dma(self, queue, out, in_, sem, waits=()):
        if self.stopped:
            return (sem, self.dcnt.get(sem, 0))
        ws = self._filter(queue, list(waits))
        self.dcnt[sem] = self.dcnt.get(sem, 0) + 16
        v = self.dcnt[sem]
        semh = self.semh

        def th(e):
            for s, vv in ws:
                e.wait_ge(semh[s], vv)
            e.dma_start(out=out, in_=in_).then_inc(semh[sem], 16)
        self.ops[queue].append(th)
        return (sem, v)

    def now(self, eng):
        return (eng, self.cnt[eng])

    def barrier(self):
        cur = [self.now(e) for e in ("pe", "act", "dve")]
        for e in ("pe", "act", "dve"):
            self.pending[e] = self.pending.get(e, []) + [w for w in cur if w[0] != e]


def build_program(stop=None):
    nc = bass.Bass("TRN2", target_bir_lowering=False)
    dumps = []

    def chk(name, views):
        if stop == name and not P.stopped:
            for nm, v, dt_ in views:
                dumps.append((nm, v, dt_, [P.now("pe"), P.now("act"), P.now("dve")]))
            P.stopped = True
    xs = nc.dram_tensor("xs", [3, S, D], F32, kind="ExternalInput").ap()
    c3t = nc.dram_tensor("c3t", [128, 16, 3], F32, kind="ExternalInput").ap()
    wts = nc.dram_tensor("wts", [NW, 128, SLOT_ELEMS], F32, kind="ExternalInput").ap()
    dft = nc.dram_tensor("dft", [NDFT, 128, SLOT_ELEMS], BF, kind="ExternalInput").ap()
    ident_d = nc.dram_tensor("ident", [128, 128], BF, kind="ExternalInput").ap()
    btab_d = nc.dram_tensor("btab", [128, 5, 512], BF, kind="ExternalInput").ap()
    identg_d = nc.dram_tensor("identg", [128, 4, 128], BF, kind="ExternalInput").ap()
    cc_d = nc.dram_tensor("cctab", [128, 2, 2, 256], BF, kind="ExternalInput").ap()
    gvec_d = nc.dram_tensor("gvec", [128, 2, 16], F32, kind="ExternalInput").ap()
    ccpos_d = nc.dram_tensor("ccpos", [128, 2, 256], BF, kind="ExternalInput").ap()
    altc_d = nc.dram_tensor("altc", [128, 2], BF, kind="ExternalInput").ap()
    bmodT_d = nc.dram_tensor("bmodT", [128, 64], F32, kind="ExternalInput").ap()
    bmodg_d = nc.dram_tensor("bmodg", [1, 4096], F32, kind="ExternalInput").ap()
    bgate_d = nc.dram_tensor("bgate", [128, 32], F32, kind="ExternalInput").ap()
    sink_d = nc.dram_tensor("sink", [1, 16], F32, kind="ExternalInput").ap()
    gfin_d = nc.dram_tensor("gfin", [1, D], F32, kind="ExternalInput").ap()
    y = nc.dram_tensor("y", [5120, D], F32, kind="ExternalOutput").ap()
    gates_d = nc.dram_tensor("gates_scr", [3, 4096], F32, kind="Internal").ap()

    P = Plan()
    OP = P.op
    ARENA_BYTES = 207 * KB
    stack = ExitStack()
    arena = stack.enter_context(nc.sbuf_tensor("arena", [128, ARENA_BYTES // 2], BF))
    ps = stack.enter_context(nc.psum_tensor("psum", [128, 4096], F32))

    def V(off, dtype, shape):
        esz = 4 if dtype == F32 else 2
        n = 1
        for s_ in shape[1:]:
            n *= s_
        nb = n * esz
        assert off % 4 == 0 and off + nb <= ARENA_BYTES, (off, nb)
        v = arena[:, off // 2: (off + nb) // 2]
        if dtype == F32:
            v = v.bitcast(F32)
        if len(shape) == 3:
            v = v.rearrange("p (a b) -> p a b", a=shape[1])
        elif len(shape) == 4:
            v = v.rearrange("p (a b c) -> p a b c", a=shape[1], b=shape[2])
        return v

    o = 0
    ringv = []
    for i in range(NS):
        ringv.append(V(o, BF, [128, SLOT_ELEMS]))
        o += 8 * KB
    ident = V(o, BF, [128, 128]); o += 256
    ones = V(o, BF, [128, 128]); o += 256
    cctab = V(o, BF, [128, 2, 2, 256]); o += 2048
    esink = V(o, F32, [128, 16]); o += 64
    bgate = V(o, F32, [128, 32]); o += 128
    gvec = V(o, F32, [128, 2, 16]); o += 128
    bmodT = V(o, F32, [128, 64]); o += 256
    modsb = V(o, F32, [128, 64, 3]); o += 768
    gmod = V(o, F32, [128, 2, 3, 16]); o += 384
    stat = V(o, F32, [128, 16]); o += 64
    scT = V(o, BF, [128, 16, 3]); o += 128
    gfin = V(o, F32, [128, D]); o += 8 * KB
    gates = V(o, F32, [128, 2, D]); o += 16 * KB
    kv_off = o
    kT = V(o, BF, [128, 4, S]); o += 16 * KB
    vv = V(o, BF, [128, 16, 512]); o += 16 * KB
    YT = V(o, BF, [128, 8, S]); o += 32 * KB
    hT_off = o
    hT = V(o, BF, [128, 16, 512]); o += 16 * KB
    xnew_off = o
    xnew = V(o, F32, [128, 4, D]); o += 32 * KB
    A0 = o
    assert ARENA_BYTES - A0 >= 32 * KB, (ARENA_BYTES - A0)
    uu = V(xnew_off, BF, [128, 16, 1024])
    atbt = V(hT_off, BF, [128, 16, 512])
    attnT = V(xnew_off, BF, [128, 16, 512])
    xst = [V(A0, F32, [128, D]), V(A0 + 8 * KB, F32, [128, D]), V(A0 + 24 * KB, F32, [128, D])]
    xn = [V(A0 + 16 * KB, BF, [128, D]), V(A0 + 20 * KB, BF, [128, D])]
    junk = V(A0 + 24 * KB, BF, [128, D])
    qT = [V(A0, BF, [128, 4, 512]), V(A0 + 4 * KB, BF, [128, 4, 512])]
    pT = [V(A0 + 8 * KB, BF, [128, 3, 512]), V(A0 + 11 * KB, BF, [128, 3, 512])]
    den = [V(A0 + 14 * KB, F32, [128, 512]), V(A0 + 16 * KB, F32, [128, 512])]
    rden = [V(A0 + 18 * KB, F32, [128, 512]), V(A0 + 20 * KB, F32, [128, 512])]
    btab = V(A0 + 26 * KB, BF, [128, 5, 512])
    identg = V(A0 + 31 * KB, BF, [128, 4, 128])
    mergedT = V(A0, BF, [128, 16, 512])
    gA = V(A0 + 16 * KB, F32, [128, 4, 512])
    gF = V(A0 + 24 * KB, F32, [128, 4, 512])
    tmpb = [V(A0 + 16 * KB + 2 * KB * q, F32, [128, 512]) for q in range(4)]
    xn2 = [V(A0 + 24 * KB, BF, [128, D]), V(A0 + 28 * KB, BF, [128, D])]
    actT = V(A0, BF, [128, 24, 512])
    sgx = [V(A0 + 24 * KB + 2 * KB * q, F32, [128, 512]) for q in range(4)]
    junk3 = V(hT_off, BF, [128, D])
    dbuf = V(A0, BF, [128, 16, 514])
    ccpos = V(A0 + 17 * KB, BF, [128, 2, 256])
    altc = V(A0 + 18 * KB, BF, [128, 2])
    col512 = V(A0 + 18 * KB + 64, BF, [128, 16])
    scbc = V(A0, BF, [128, 16, 3, 128])
    bmodg = V(A0 + 12 * KB, F32, [128, 4096])
    sinkb = V(A0 + 28 * KB, F32, [128, 16])
    c3sb = V(A0 + 29 * KB, F32, [128, 16, 3])
    gst = V(kv_off, F32, [128, 3 * 4096])

    def bank(b):
        return ps[:, b * 512:(b + 1) * 512]

    def bank4(b):
        return bank(b).rearrange("p (a b) -> p a b", a=4)

    def bankT(b0):
        return ps[:, b0 * 512:(b0 + 2) * 512].bitcast(BF).rearrange("p (a b) -> p a b", a=16)

    bank_free = {b: [] for b in range(8)}
    ring_entries = []

    def ring_acquire(src):
        i = len(ring_entries)
        slot = i % NS
        ring_entries.append([src, slot, None, P.stopped])
        return i, slot, ("ring%d" % slot, 16 * (i // NS + 1))

    def ring_release(i, w):
        ring_entries[i][2] = w

    def wsrc(name, idx):
        return wts[_off[name] + idx]

    def mm_group(srcs, nk, mode, act_fn, banks, evac, in_waits=(), c4list=(0, 1, 2, 3)):
        ready = {}
        first = True
        for si, src in enumerate(srcs):
            i, slot, w = ring_acquire(src)
            nkk = min(8, nk - si * 8)
            rv = ringv[slot].rearrange("p (k n) -> p k n", k=8)
            c = None
            for ci4, c4 in enumerate(c4list):
                for kk in range(nkk):
                    k = si * 8 + kk
                    if mode == "W":
                        lhsT = rv[:, kk, c4 * 128:(c4 + 1) * 128]
                        rhs = act_fn(k)
                    else:
                        lhsT = act_fn(k, c4)
                        rhs = rv[:, kk, :]
                    waits = []
                    if ci4 == 0 and kk == 0:
                        waits.append(w)
                    if first:
                        waits += list(in_waits)
                        first = False
                    if k == 0:
                        waits += bank_free[banks[ci4]]
                        bank_free[banks[ci4]] = []
                    last_slot = (ci4 == len(c4list) - 1 and kk == nkk - 1)
                    last_bank = (k == nk - 1)
                    c = OP("pe", "matmul", dict(out=bank(banks[ci4]), lhsT=lhsT, rhs=rhs, start=(k == 0),
                                                 stop=(k == nk - 1)), waits, inc=(last_slot or last_bank))
                    if last_bank:
                        ready[c4] = c
            ring_release(i, c)
        for ci4, c4 in enumerate(c4list):
            bank_free[banks[ci4]] = list(evac(c4, bank(banks[ci4]), ready[c4]))

    xbuf_free = {0: [], 1: [], 2: []}
    import collections as _c
    xn_free = _c.defaultdict(list)
    tog = {"n": 0}

    pref = {}

    def x_prefetch(key, srows, extra_waits):
        lds = {}
        for j in range(3):
            ew = [w for w in extra_waits if (j == 2 or w[0] == "pe")]
            lds[j] = P.dma("sp", xst[j], srows(j), "xld%d" % j, waits=xbuf_free[j] + ew)
            xbuf_free[j] = []
        pref["key"] = key
        pref["lds"] = lds

    def norm_sub(j, xin, lw, gm, sh, dstT, xnb, dst_waits=(), on_stats=None):
        col = stat[:, j:j + 1]
        a1 = OP("act", "activation", dict(out=xnb, in_=xin, func=AF.Square, accum_out=col),
                list(lw) + list(dst_waits) + xn_free[id(xnb)])
        d1 = OP("act", "activation", dict(out=col, in_=col, func=AF.Sqrt, scale=1.0 / D, bias=EPS), [a1])
        d2 = OP("dve", "reciprocal", dict(out=col, in_=col), [d1])
        d3 = OP("dve", "tensor_scalar", dict(out=xnb[:, 0:1024], in0=xin[:, 0:1024], scalar1=col,
                                              scalar2=None, op0=ALU.mult), [d2] + list(lw))
        a3 = OP("act", "activation", dict(out=xnb[:, 1024:2048], in_=xin[:, 1024:2048], func=AF.Identity,
                                          scale=col), [d2])
        if on_stats is not None:
            on_stats([d3, a3])

        def emit():
            tb = 2 * (tog["n"] % 2)
            tog["n"] += 1
            pt = bankT(tb)
            w0 = [d3, a3] + bank_free[tb] + bank_free[tb + 1]
            bank_free[tb] = []
            bank_free[tb + 1] = []
            t = None
            for f in range(16):
                t = OP("pe", "transpose", dict(out=pt[:, f, :], in_=xnb[:, f * 128:(f + 1) * 128],
                                               identity=ident), w0 if f == 0 else (), inc=(f == 15))
            xn_free[id(xnb)] = [t]
            ev = None
            for f in range(16):
                dst = dstT[:, f, j * 128:(j + 1) * 128]
                ev = OP("dve", "tensor_scalar", dict(out=dst, in0=pt[:, f, :], scalar1=gm(f), scalar2=sh(f),
                                                      op0=ALU.mult, op1=ALU.add), [t] + list(dst_waits))
            bank_free[tb] = [ev]
            bank_free[tb + 1] = [ev]
        return emit

    def norm_stage(srows, src_sb, gm, sh, dstT, xbuf, xnbuf, jbuf, dst_waits=(), key=None):
        prev_em = None
        for j in range(4):
            b = j % 2
            if srows is not None:
                xb = j % 3
                if key is not None and pref.get("key") == key and j in pref["lds"]:
                    ld = pref["lds"].pop(j)
                else:
                    ld = P.dma("sp", xbuf[xb], srows(j), "xld%d" % xb, waits=xbuf_free[xb])
                xin = xbuf[xb]
                lw = [ld]

                def rel(ws, xb=xb):
                    xbuf_free[xb] = list(ws)
                em = norm_sub(j, xin, lw, gm, sh, dstT, xnbuf[b], dst_waits, on_stats=rel)
            else:
                em = norm_sub(j, src_sb(j), [], gm, sh, dstT, xnbuf[b], dst_waits)
            if prev_em is not None:
                prev_em()
            prev_em = em
        prev_em()

    def copy_evac(dst_fn, flip=0, guard=False, dve_only=False):
        def ev(c4, bk, rdy):
            dst = dst_fn(c4)
            gw = list(xguard) if guard else []
            if (c4 + flip) % 2 == 0 and not dve_only:
                return [OP("act", "activation", dict(out=dst, in_=bk, func=AF.Copy), [rdy] + gw)]
            return [OP("dve", "tensor_copy", dict(out=dst, in_=bk), [rdy] + gw)]
        return ev

    cl = None
    for dst, src in ((ident, ident_d), (cctab, cc_d), (gvec, gvec_d), (bmodT, bmodT_d),
                     (bgate, bgate_d), (gfin, gfin_d[0:1, :].partition_broadcast(128)),
                     (bmodg, bmodg_d[0:1, :].partition_broadcast(128)),
                     (sinkb, sink_d[0:1, :].partition_broadcast(128)), (c3sb, c3t)):
        cl = P.dma("sp", dst, src, "cld")
    cw = [cl]
    a_sc = OP("act", "activation", dict(out=scT, in_=c3sb, func=AF.Silu), cw)
    OP("act", "activation", dict(out=esink, in_=sinkb, func=AF.Exp), cw)
    OP("dve", "memset", dict(ap=ones, constant=1.0), cw)
    dsc = None
    for s_ in range(3):
        dsc = OP("dve", "tensor_copy", dict(out=scbc[:, :, s_, :],
                                             in_=scT[:, :, s_:s_ + 1].to_broadcast([128, 16, 128])), [a_sc])
    modps = bank(0).rearrange("p (a b) -> p a b", a=128)

    def modf_part(which):
        pe_last = None
        for si in range(16 * which, 16 * which + 16):
            i, slot, w = ring_acquire(wsrc("w_modf", si))
            rv = ringv[slot].rearrange("p (k n) -> p k n", k=16)
            c = None
            for fl in range(2):
                f = si * 2 + fl
                for k in range(16):
                    waits = []
                    if fl == 0 and k == 0:
                        waits = [w, a_sc]
                        if si == 16 * which:
                            waits += bank_free[0]
                            bank_free[0] = []
                    c = OP("pe", "matmul", dict(out=modps[:, f, 0:3], lhsT=rv[:, k, fl * 128:(fl + 1) * 128],
                                                 rhs=scT[:, k, :], start=(k == 0), stop=(k == 15)),
                           waits, inc=(fl == 1 and k == 15))
            ring_release(i, c)
            pe_last = c
        f0 = 32 * which
        dm = None
        for s_ in range(3):
            dm = OP("dve", "tensor_tensor", dict(out=modsb[:, f0:f0 + 32, s_], in0=modps[:, f0:f0 + 32, s_],
                                                 in1=bmodT[:, f0:f0 + 32], op=ALU.add), [pe_last] + cw)
        for s_ in range(3):
            dm = OP("dve", "scalar_tensor_tensor", dict(out=gmod[:, which, s_, :],
                                                        in0=modsb[:, f0 + 16:f0 + 32, s_], scalar=1.0,
                                                        in1=gvec[:, which, :], op0=ALU.add, op1=ALU.mult), [dm])
        bank_free[0] = [dm]
        return dm

    mod_ready = modf_part(0)
    mod2 = {}
    gl = None
    gbanks = [4, 5, 6]
    for cb in range(8):
        rdy = [None] * 3
        for si in range(2):
            i, slot, w = ring_acquire(wsrc("w_modg", cb * 2 + si))
            rv = ringv[slot].rearrange("p (k n) -> p k n", k=8)
            c = None
            for s_ in range(3):
                for kk in range(8):
                    k = si * 8 + kk
                    waits = []
                    if s_ == 0 and kk == 0:
                        waits += [w, dsc]
                    if k == 0:
                        waits += bank_free[gbanks[s_]]
                        bank_free[gbanks[s_]] = []
                    c = OP("pe", "matmul", dict(out=bank(gbanks[s_]), lhsT=scbc[:, k, s_, :], rhs=rv[:, kk, :],
                                                 start=(k == 0), stop=(k == 15)), waits,
                           inc=((s_ == 2 and kk == 7) or k == 15))
                    if k == 15:
                        rdy[s_] = c
            ring_release(i, c)
        for s_ in range(3):
            gl = OP("dve", "tensor_tensor", dict(out=gst[0:1, s_ * 4096 + cb * 512: s_ * 4096 + (cb + 1) * 512],
                                                 in0=bank(gbanks[s_])[0:1, :],
                                                 in1=bmodg[0:1, cb * 512:(cb + 1) * 512], op=ALU.add),
                    [rdy[s_]] + cw)
            bank_free[gbanks[s_]] = [gl]
    gstore = P.dma("sp", gates_d.rearrange("(o s) n -> o (s n)", o=1), gst[0:1, :], "gst", waits=[gl])
    chk("prologue", [("modsb", modsb, F32), ("gmod", gmod, F32), ("gst", gst[0:1, :], F32), ("esink", esink, F32)])

    sb_free = {0: [], 1: []}
    pT_free = {0: [], 1: []}
    den_free = {0: [], 1: []}
    qfree = {0: [], 1: []}
    g_free = {"A": [], "F": []}
    sg_free = {0: [], 1: [], 2: [], 3: []}
    yst_free = {0: [], 1: []}
    SUMB = 7
    xguard = []
    ystores_last = []
    tctr = [0]
    tmp_free = {0: [], 1: [], 2: [], 3: []}

    def take_guard():
        g_ = list(xguard)
        del xguard[:]
        return g_

    for sl in range(3):
        n_own = 4 if sl < 2 else 2
        half = (sl == 2)
        yrow0 = sl * 2048
        P.barrier()
        gld = P.dma("sp", gates.rearrange("p a b -> p (a b)"), gates_d[sl:sl + 1, :].partition_broadcast(128),
                    "gld", waits=[gstore, P.now("dve"), P.now("act"), P.now("pe")])

        def gm1(f, sl=sl):
            return gmod[:, 0, sl, f:f + 1]

        def sh1(f, sl=sl):
            return modsb[:, f, sl:sl + 1]

        def gm2(f, sl=sl):
            return gmod[:, 1, sl, f:f + 1]

        def sh2(f, sl=sl):
            return modsb[:, 32 + f, sl:sl + 1]

        def xrows(t):
            return lambda j: xs[sl, t * 512 + j * 128: t * 512 + (j + 1) * 128, :]

        for t in range(4):
            P.barrier()
            norm_stage(xrows(t), None, gm1, sh1, hT, xst, xn, junk, dst_waits=[mod_ready], key=("pro", sl, t))
            hready = [P.now("dve"), P.now("act")]
            chk("n0", [("hT", hT, BF), ("xn", xn[1], BF), ("stat", stat, F32)])
            mm_group([wsrc("w_in", 8), wsrc("w_in", 9)], 16, "W", lambda k: hT[:, k, :], [4, 5, 6, 7],
                     copy_evac(lambda c4: kT[:, c4, t * 512:(t + 1) * 512]), in_waits=hready)
            chk("k0", [("kT", kT, BF)])
            mm_group([wsrc("w_in", 10), wsrc("w_in", 11)], 16, "A", lambda k, j: hT[:, k, j * 128:(j + 1) * 128],
                     [0, 1, 2, 3], copy_evac(lambda c4: vv[:, t * 4 + c4, :], 1))
            for uh in range(2):
                mm_group([wsrc("w_in", 12 + uh * 2), wsrc("w_in", 13 + uh * 2)], 16, "A",
                         lambda k, j: hT[:, k, j * 128:(j + 1) * 128], [4, 5, 6, 7] if uh == 0 else [0, 1, 2, 3],
                         copy_evac(lambda c4: uu[:, t * 4 + c4, uh * 512:(uh + 1) * 512], uh, guard=True))
            if sl == 0 and t == 0:
                mod2["r"] = modf_part(1)
        P.barrier()
        chk("kvu", [("kT", kT, BF), ("vv", vv, BF), ("uu", uu, BF), ("hT", hT, BF)])
        def pos_dft(base, dst):
            for tb_ in range(2):
                e0 = ring_acquire(dft[base + tb_ * 2])
                e1 = ring_acquire(dft[base + tb_ * 2 + 1])
                rvs = [ringv[e0[1]].rearrange("p (k n) -> p k n", k=8),
                       ringv[e1[1]].rearrange("p (k n) -> p k n", k=8)]
                c = None
                for cc in range(8):
                    bk = cc
                    for sc in range(16):
                        waits = []
                        if cc == 0 and sc == 0:
                            waits.append(e0[2])
                        if cc == 0 and sc == 8:
                            waits.append(e1[2])
                        if sc == 0:
                            waits += bank_free[bk]
                            bank_free[bk] = []
                        c = OP("pe", "matmul", dict(out=bank(bk), lhsT=uu[:, sc, cc * 128:(cc + 1) * 128],
                                                     rhs=rvs[sc // 8][:, sc % 8, :], start=(sc == 0), stop=(sc == 15)),
                               waits, inc=(sc == 15 or (cc == 7 and sc == 7)))
                        if cc == 7 and sc == 7:
                            ring_release(e0[0], c)
                    d_ = dst(tb_, cc)
                    if cc % 2 == 0:
                        r = OP("act", "activation", dict(out=d_, in_=bank(bk), func=AF.Copy), [c])
                    else:
                        r = OP("dve", "tensor_copy", dict(out=d_, in_=bank(bk)), [c])
                    bank_free[bk] = [r]
                ring_release(e1[0], c)

        def chan_dft(rhs, mirror, kt):
            evw = [P.now("act"), P.now("dve")]
            for g in range(4):
                for mc in range(2):
                    oc = g * 2 + mc
                    bk = oc
                    n = 0
                    c = None
                    for tb_ in range(2):
                        for ci in range(2):
                            waits = []
                            if n == 0:
                                waits = list(evw) + bank_free[bk]
                                bank_free[bk] = []
                            if mirror and tb_ == 1:
                                lt = ccpos[:, ci, mc * 128:(mc + 1) * 128]
                            else:
                                lt = cctab[:, tb_, ci, mc * 128:(mc + 1) * 128]
                            c = OP("pe", "matmul", dict(out=bank(bk), lhsT=lt, rhs=rhs(tb_ * 8 + g * 2 + ci),
                                                         start=(n == 0), stop=(n == 3)), waits, inc=(n == 3))
                            n += 1
                    d_ = YT[:, oc, kt * 512:(kt + 1) * 512]
                    if oc % 2 == 0:
                        r = OP("act", "activation", dict(out=d_, in_=bank(bk), func=AF.Copy), [c])
                    else:
                        r = OP("dve", "tensor_copy", dict(out=d_, in_=bank(bk)), [c])
                    bank_free[bk] = [r]

        if half:
            for kt in range(n_own):
                pos_dft(16 + kt * 4, lambda tb_, cc: atbt[:, tb_ * 8 + cc, :])
                chan_dft(lambda ch: atbt[:, ch, :], False, kt)
                P.barrier()
        else:
            sw = [P.now("act"), P.now("dve"), P.now("pe")]
            P.dma("sp", ccpos, ccpos_d, "cld", waits=sw)
            cld2 = P.dma("sp", altc, altc_d, "cld", waits=sw)
            for p_ in (1, 0):
                pos_dft(p_ * 4, lambda tb_, cc: dbuf[:, tb_ * 8 + cc, 0:512])
                if p_ == 1:
                    c = None
                    for cc in range(8):
                        for sc in range(16):
                            waits = []
                            if cc == 0 and sc == 0:
                                waits = [cld2] + bank_free[7]
                                bank_free[7] = []
                            c = OP("pe", "matmul", dict(out=bank(7)[:, cc:cc + 1], lhsT=uu[:, sc, cc * 128:(cc + 1) * 128],
                                                         rhs=altc[:, 0:1], start=(sc == 0), stop=(sc == 15)),
                                   waits, inc=(cc == 7 and sc == 15))
                    aw = [P.now("act")]
                    r1 = OP("dve", "tensor_copy", dict(out=dbuf[:, 0:8, 512], in_=bank(7)[:, 0:8]), [c] + aw)
                    OP("dve", "memset", dict(ap=dbuf[:, 8:16, 512], constant=0.0), aw)
                    OP("dve", "tensor_copy", dict(out=col512, in_=dbuf[:, :, 0]), aw + [P.now("dve")])
                    bank_free[7] = [r1]
                else:
                    OP("dve", "tensor_copy", dict(out=dbuf[:, :, 512], in_=col512), [P.now("act"), P.now("dve")])
                chan_dft(lambda ch: dbuf[:, ch, 0:512], False, p_)
                chan_dft(lambda ch: dbuf[:, ch, 512:0:-1], True, 3 - p_)
                P.barrier()
            for b_ in (0, 1, 2):
                xbuf_free[b_] = xbuf_free[b_] + [P.now("pe"), P.now("dve"), P.now("act")]

        chk("dft", [("YT", YT, BF)])
        for t in range(n_own):
            P.barrier()
            norm_stage(xrows(t), None, gm1, sh1, hT, xst, xn, junk, key=("tile", sl, t))
            P.barrier()
            chk("s0", [("hT", hT, BF)])
            bl = P.dma("sp", btab, btab_d, "cld", waits=[P.now("act"), P.now("dve"), P.now("pe")])
            bl = P.dma("sp", identg, identg_d, "cld", waits=[P.now("act"), P.now("dve"), P.now("pe")])
            unit = 0
            pend = []

            def flush_pv():
                while pend:
                    (g_, qb_i, rs_, ub_, exps_, pTu_, denu_) = pend.pop(0)
                    rdenu_ = rden[ub_]
                    OB = 3 + ub_
                    SB_ = 5 + ub_
                    nr = len(rs_)
                    c = None
                    for idx, (r, kb, di) in enumerate(rs_):
                        waits = [exps_[idx]]
                        if idx == 0:
                            waits += bank_free[SB_]
                            bank_free[SB_] = []
                        c = OP("pe", "matmul", dict(out=bank(SB_), lhsT=ones, rhs=pTu_[:, r, :], start=(idx == 0),
                                                     stop=(idx == nr - 1)), waits, inc=(idx == nr - 1))
                    c_sum = c
                    for idx, (r, kb, di) in enumerate(rs_):
                        waits = []
                        if idx == 0:
                            waits += bank_free[OB]
                            bank_free[OB] = []
                        c = OP("pe", "matmul", dict(out=bank(OB), lhsT=vv[:, kb, g_ * 128:(g_ + 1) * 128],
                                                     rhs=pTu_[:, r, :], start=(idx == 0), stop=(idx == nr - 1)),
                               waits, inc=(idx == nr - 1))
                    c_out = c
                    pT_free[ub_] = [c_out]
                    d1 = OP("dve", "tensor_tensor",
                            dict(out=denu_.rearrange("p (a b) -> p a b", a=4), in0=bank4(SB_),
                                 in1=esink[:, 4 * g_:4 * g_ + 4].unsqueeze(2).to_broadcast([128, 4, 128]),
                                 op=ALU.add), [c_sum] + den_free[ub_])
                    l1 = OP("act", "activation", dict(out=rdenu_, in_=denu_, func=AF.Ln), [d1])
                    d2 = OP("act", "activation", dict(out=rdenu_, in_=rdenu_, func=AF.Exp, scale=-1.0), [l1])
                    d3 = OP("dve", "tensor_tensor",
                            dict(out=attnT[:, 4 * g_:4 * g_ + 4, qb_i * 128:(qb_i + 1) * 128], in0=bank4(OB),
                                 in1=rdenu_.rearrange("p (a b) -> p a b", a=4), op=ALU.mult),
                            [d2, c_out] + take_guard())
                    bank_free[OB] = [d3]
                    bank_free[SB_] = [d1]
                    den_free[ub_] = [d3]

            for g in range(4):
                qb_ = qT[g % 2]
                mm_group([wsrc("w_in", g * 2), wsrc("w_in", g * 2 + 1)], 16, "W", lambda k: hT[:, k, :],
                         [4, 5, 6, 7], copy_evac(lambda c4: qb_[:, c4, :], dve_only=True), in_waits=qfree[g % 2])
                qready = [P.now("act"), P.now("dve")]
                for qb in range(4):
                    i_loc = t * 4 + qb
                    rs = []
                    for r in range(3):
                        kbi = i_loc + r - 1
                        if half:
                            di = r
                            if r == 0 and i_loc == 0:
                                di = 3
                            if r == 2 and i_loc == 7:
                                di = 4
                            rs.append((r, kbi % 16, di))
                        elif 0 <= kbi < 16:
                            rs.append((r, kbi, r))
                    ub = unit % 2
                    sbanks = [0, 1, 2]
                    unit += 1
                    pTu, denu = pT[ub], den[ub]
                    c = None
                    for (r, kb, di) in rs:
                        bk = sbanks[r]
                        waits = list(qready) + bank_free[bk] + [bl]
                        bank_free[bk] = []
                        OP("pe", "matmul", dict(out=bank4(bk), lhsT=kT[:, g, kb * 128:(kb + 1) * 128],
                                                 rhs=qb_[:, :, qb * 128:(qb + 1) * 128], start=True, stop=False),
                           waits, inc=False)
                        c = OP("pe", "matmul", dict(out=bank(bk), lhsT=identg[:, g, :], rhs=btab[:, di, :],
                                                     start=False, stop=True), inc=(r == rs[-1][0]))
                    r0_, r1_ = rs[0][0], rs[-1][0] + 1
                    ex = OP("act", "activation",
                            dict(out=pTu[:, r0_:r1_, :],
                                 in_=ps[:, r0_ * 512:r1_ * 512].rearrange("p (a b) -> p a b", a=r1_ - r0_),
                                 func=AF.Exp, scale=SCALE), [c] + pT_free[ub])
                    exps = [ex] * len(rs)
                    for (r, kb, di) in rs:
                        bank_free[sbanks[r]] = [ex]
                    if qb == 3:
                        qfree[g % 2] = [c]
                    flush_pv()
                    pend.append((g, qb, rs, ub, exps, pTu, denu))
            flush_pv()
            P.barrier()
            chk("s1", [("attnT", attnT, BF)])
            xl = {}
            for j in (2, 3):
                xl[j] = P.dma("sp", xnew[:, j, :], xs[sl, t * 512 + j * 128: t * 512 + (j + 1) * 128, :],
                              "x2ld%d" % (j % 2), waits=[P.now("dve"), P.now("act"), P.now("pe")] + ystores_last)
            for cg in range(4):
                sigA, mulA, sigF, fin = [], [], [], []

                def sigA_evac(c4, bk, rdy):
                    r = OP("act", "activation", dict(out=gA[:, c4, :], in_=bk, func=AF.Sigmoid,
                                                     bias=bgate[:, cg * 4 + c4: cg * 4 + c4 + 1]),
                           [rdy] + g_free["A"])
                    sigA.append(r)
                    return [r]

                def mulA_evac(c4, bk, rdy):
                    r = OP("dve", "tensor_tensor", dict(out=gA[:, c4, :], in0=bk, in1=gA[:, c4, :], op=ALU.mult),
                           [rdy, sigA[c4]])
                    mulA.append(r)
                    return [r]

                def sigF_evac(c4, bk, rdy):
                    r = OP("act", "activation", dict(out=gF[:, c4, :], in_=bk, func=AF.Sigmoid,
                                                     bias=bgate[:, 16 + cg * 4 + c4: 16 + cg * 4 + c4 + 1]),
                           [rdy] + g_free["F"])
                    sigF.append(r)
                    return [r]

                def mulF_evac(c4, bk, rdy):
                    r1 = OP("dve", "tensor_tensor", dict(out=gF[:, c4, :], in0=bk, in1=gF[:, c4, :], op=ALU.mult),
                            [rdy, sigF[c4]])
                    r2 = OP("dve", "tensor_tensor", dict(out=mergedT[:, cg * 4 + c4, :], in0=gA[:, c4, :],
                                                          in1=gF[:, c4, :], op=ALU.add), [r1, mulA[c4]])
                    fin.append(r2)
                    return [r1]
                mm_group([wsrc("w_gate", cg * 2), wsrc("w_gate", cg * 2 + 1)], 16, "W", lambda k: hT[:, k, :],
                         [0, 1, 2, 3], sigA_evac)
                mm_group([wsrc("w_attn", cg * 2), wsrc("w_attn", cg * 2 + 1)], 16, "W", lambda k: attnT[:, k, :],
                         [4, 5, 6, 7], mulA_evac)
                mm_group([wsrc("w_gate", (4 + cg) * 2), wsrc("w_gate", (4 + cg) * 2 + 1)], 16, "W",
                         lambda k: hT[:, k, :], [0, 1, 2, 3], sigF_evac)
                mm_group([wsrc("w_four", cg)], 8, "W", lambda k: YT[:, k, t * 512:(t + 1) * 512], [4, 5, 6, 7],
                         mulF_evac)
                g_free["A"] = [fin[-1]]
                g_free["F"] = [fin[-1]]
            P.barrier()
            chk("s3", [("mergedT", mergedT, BF)])
            for j in (0, 1):
                xl[j] = P.dma("sp", xnew[:, j, :], xs[sl, t * 512 + j * 128: t * 512 + (j + 1) * 128, :],
                              "x2ld%d" % (j % 2), waits=[P.now("dve"), P.now("act"), P.now("pe")] + ystores_last)
            emits = []
            ti = 0
            for ph, subs in enumerate(((2, 3), (0, 1))):
                for cb in range(4):
                    def res_evac(c4, bk, rdy):
                        nonlocal_ti = tctr[0]
                        tctr[0] += 1
                        tb_ = tmpb[nonlocal_ti % 4]
                        r1 = OP("dve", "tensor_tensor", dict(out=tb_, in0=bk, in1=gates[:, 0, cb * 512:(cb + 1) * 512],
                                                             op=ALU.mult), [rdy, gld] + tmp_free[nonlocal_ti % 4])
                        r2 = OP("dve", "tensor_tensor", dict(out=xnew[:, c4, cb * 512:(cb + 1) * 512],
                                                             in0=xnew[:, c4, cb * 512:(cb + 1) * 512], in1=tb_,
                                                             op=ALU.add), [r1, xl[c4]])
                        tmp_free[nonlocal_ti % 4] = [r2]
                        return [r1]
                    mm_group([wsrc("w_out", cb * 2), wsrc("w_out", cb * 2 + 1)], 16, "A",
                             lambda k, j: mergedT[:, k, j * 128:(j + 1) * 128],
                             [4, 5] if cb % 2 == 0 else [6, 7], res_evac, c4list=subs)
                    if ph == 1 and cb in (1, 2):
                        emits[cb - 1]()
                if ph == 0:
                    for qi, j in enumerate((2, 3)):
                        emits.append(norm_sub(j, xnew[:, j, :], [P.now("dve")], gm2, sh2, hT, xn2[qi],
                                              dst_waits=[mod2["r"]]))
            em0 = norm_sub(0, xnew[:, 0, :], [P.now("dve")], gm2, sh2, hT, xn2[0])
            em1 = norm_sub(1, xnew[:, 1, :], [P.now("dve")], gm2, sh2, hT, xn2[1])
            em0()
            em1()
            P.barrier()
            chk("s4", [("xnew", xnew, F32)])
            chk("s5", [("h2T", hT, BF)])
            act_free = []
            for hf in range(2):
                cbs = list(range(0, 6)) if hf == 0 else list(range(6, 11))
                nch = 24 if hf == 0 else 20
                for ci_, cb in enumerate(cbs):
                    sgr = [None] * 4

                    def silu_evac(c4, bk, rdy):
                        r = OP("act", "activation", dict(out=sgx[c4], in_=bk, func=AF.Silu), [rdy] + sg_free[c4])
                        sgr[c4] = r
                        return [r]

                    def up_evac(c4, bk, rdy):
                        r = OP("dve", "tensor_tensor", dict(out=actT[:, ci_ * 4 + c4, :], in0=bk, in1=sgx[c4],
                                                            op=ALU.mult), [rdy, sgr[c4]] + act_free)
                        sg_free[c4] = [r]
                        return [r]
                    mm_group([wsrc("w_up", cb * 2), wsrc("w_up", cb * 2 + 1)], 16, "W", lambda k: hT[:, k, :],
                             [0, 1, 2, 3], silu_evac)
                    mm_group([wsrc("w_up", (11 + cb) * 2), wsrc("w_up", (11 + cb) * 2 + 1)], 16, "W",
                             lambda k: hT[:, k, :], [4, 5, 6, 7], up_evac)
                aready = [P.now("dve")]
                for cb in range(4):
                    def down_evac(c4, bk, rdy):
                        r1 = OP("dve", "tensor_tensor", dict(out=sgx[c4], in0=bk,
                                                             in1=gates[:, 1, cb * 512:(cb + 1) * 512], op=ALU.mult),
                                [rdy] + sg_free[c4])
                        r2 = OP("dve", "tensor_tensor", dict(out=xnew[:, c4, cb * 512:(cb + 1) * 512],
                                                             in0=xnew[:, c4, cb * 512:(cb + 1) * 512], in1=sgx[c4],
                                                             op=ALU.add), [r1])
                        sg_free[c4] = [r2]
                        return [r1]
                    base = hf * 12 + cb * 3
                    mm_group([wsrc("w_down", base), wsrc("w_down", base + 1), wsrc("w_down", base + 2)], nch, "A",
                             lambda k, j: actT[:, k, j * 128:(j + 1) * 128],
                             [0, 1, 2, 3] if cb % 2 == 0 else [4, 5, 6, 7], down_evac, in_waits=aready)
                act_free = [P.now("pe")]
            P.barrier()
            chk("s6", [("xnew", xnew, F32)])
            nxt = None
            if t + 1 < n_own:
                nxt = (("tile", sl, t + 1), xrows(t + 1))
            elif sl + 1 < 3:
                nxt = (("pro", sl + 1, 0), (lambda sl2: (lambda j: xs[sl2, j * 128:(j + 1) * 128, :]))(sl + 1))
            if nxt is not None:
                x_prefetch(nxt[0], nxt[1], [P.now("pe"), P.now("dve"), P.now("act")])
            for j in range(4):
                col = stat[:, 8 + j: 9 + j]
                a1 = OP("act", "activation", dict(out=junk3, in_=xnew[:, j, :], func=AF.Square, accum_out=col))
                d1 = OP("act", "activation", dict(out=col, in_=col, func=AF.Sqrt, scale=1.0 / D, bias=EPS), [a1])
                d2 = OP("dve", "reciprocal", dict(out=col, in_=col), [d1])
                d3 = OP("dve", "scalar_tensor_tensor", dict(out=xnew[:, j, :], in0=xnew[:, j, :], scalar=col, in1=gfin,
                                                            op0=ALU.mult, op1=ALU.mult), [d2, a1])
                r0 = yrow0 + t * 512 + j * 128
                st = P.dma("sp", y[r0:r0 + 128, :], xnew[:, j, :], "yst%d" % (j % 2), waits=[d3])
                xguard.append(st)
                if j == 0:
                    del ystores_last[:]
                ystores_last.append(st)
    P.stopped = False
    for nm, v, dt_, ws in dumps:
        dd = nc.dram_tensor("dbg_" + nm, list(v.shape), dt_, kind="ExternalOutput").ap()
        P.dma("sp", dd, v, "yst0", waits=ws)
    fin_waits = [(k_, P.dcnt[k_]) for k_ in ("yst0", "yst1") if k_ in P.dcnt]

    def fin_th(e):
        for s_, v_ in fin_waits:
            e.wait_ge(P.semh[s_], v_)
    P.ops["sp"].append(fin_th)
    for i, (src, slot, rel, dead) in enumerate(ring_entries):
        if dead:
            break
        waits = [ring_entries[i - NS][2]] if i >= NS else []
        P.dma("pool", ringv[slot], src, "ring%d" % slot, waits)

    names = ["pe", "act", "dve"] + sorted(P.dcnt.keys())
    for nm in names:
        P.semh[nm] = stack.enter_context(nc.semaphore(nm))
    assert max(P.cnt.values()) < 60000 and max(P.dcnt.values()) < 60000, (P.cnt, P.dcnt)
    with nc.Block() as block:
        @block.tensor
        def _(e):
            for th in P.ops["pe"]:
                th(e)

        @block.scalar
        def _(e):
            for th in P.ops["act"]:
                th(e)

        @block.vector
        def _(e):
            for th in P.ops["dve"]:
                th(e)

        @block.gpsimd
        def _(e):
            for th in P.ops["pool"]:
                th(e)

        @block.sync
        def _(e):
            for th in P.ops["sp"]:
                th(e)
    stack.close()
    return nc


def _fmtA(W):
    K, N = W.shape
    KS = -(-K // 1024)
    if KS * 1024 != K:
        W = np.concatenate([W, np.zeros((KS * 1024 - K, N), W.dtype)], axis=0)
    NB = N // 512
    return np.ascontiguousarray(W.reshape(KS, 8, 128, NB, 512).transpose(3, 0, 2, 1, 4)).reshape(NB * KS, 128, 4096)


def _const_tables():
    bf = ml_dtypes.bfloat16
    s = np.arange(2048, dtype=np.int64)
    ident = np.eye(128, dtype=np.float32).astype(bf)
    m = np.arange(256, dtype=np.int64)
    ang = 2.0 * np.pi * ((m[:, None] * m[None, :]) % 256) / 256.0
    cc = np.stack([np.cos(ang) / 16.0, -np.sin(ang) / 16.0], 0)
    cctab = cc.reshape(2, 2, 128, 256).transpose(2, 0, 1, 3).astype(np.float32).astype(bf)
    ccpos = np.ascontiguousarray((np.sin(ang) / 16.0).reshape(2, 128, 256).transpose(1, 0, 2)).astype(np.float32).astype(bf)
    altc = np.stack([((-1.0) ** np.arange(128)) / np.sqrt(2048.0), np.zeros(128)], 1).astype(np.float32).astype(bf)

    def pos_tab(off):
        nk = 2048 if off is None else 1024
        o_ = 0 if off is None else off
        sg = (s + o_) % 2048
        kg = (np.arange(nk, dtype=np.int64) + o_) % 2048
        ang = 2.0 * np.pi * ((sg[:, None] * kg[None, :]) % 2048) / 2048.0
        out = []
        for kt in range(nk // 512):
            for fn in (np.cos, np.sin):
                T = (fn(ang[:, kt * 512:(kt + 1) * 512]) / np.sqrt(2048.0)).astype(np.float32)
                out.append(_fmtA(T))
        return np.concatenate(out, 0)
    full = pos_tab(None).astype(bf)
    halves = [pos_tab(0).astype(bf), pos_tab(1024).astype(bf)]
    a = np.arange(128)[None, :]
    j = np.arange(128)[:, None]
    prev = (a - j + 128).astype(np.float32)
    prev = np.where(prev <= 128, prev, BIG)
    cur = np.abs(a - j).astype(np.float32)
    nxt = (j + 128 - a).astype(np.float32)
    nxt = np.where(nxt <= 128, nxt, BIG)
    allbig = np.full((128, 128), BIG, np.float32)
    hs = (2.0 ** (-(np.arange(4) + 1) / 2.0)).astype(np.float64)

    def mk(tabs):
        t_ = np.stack(tabs, 1).astype(np.float64)
        b_ = -(t_[:, :, None, :] * hs[None, None, :, None]) / SCALE
        return b_.reshape(128, 5, 512).astype(np.float32).astype(bf)
    dist = [mk([prev, cur, nxt, allbig, nxt]), mk([prev, cur, nxt, prev, allbig])]
    identg = np.stack([np.eye(128) * (4.0 ** -g) for g in range(4)], 1).astype(np.float32).astype(bf)
    return ident, cctab, full, halves, dist, identg, ccpos, altc


def prepare(x_prompt, x_sample, c_prompt, c_sample, w_mod, b_mod, g_mix, w_in, attn_sink, w_attn_branch,
            w_fourier_branch, w_gate, b_gate, w_out, g_ffn, w_up, w_down, g_final):
    f32 = np.float32
    X = np.concatenate([np.asarray(x_prompt, f32), np.asarray(x_sample, f32)], 0)
    C = np.concatenate([np.asarray(c_prompt, f32), np.asarray(c_sample, f32)], 0)
    w_mod = np.asarray(w_mod, f32)[0]; b_mod = np.asarray(b_mod, f32)[0]
    w_in = np.asarray(w_in, f32)[0]; w_gate = np.asarray(w_gate, f32)[0]
    w_attn = np.asarray(w_attn_branch, f32)[0]; w_four = np.asarray(w_fourier_branch, f32)[0]
    w_out = np.asarray(w_out, f32)[0]; w_up = np.asarray(w_up, f32)[0]; w_down = np.asarray(w_down, f32)[0]
    g_mix = np.asarray(g_mix, f32)[0]; g_ffn = np.asarray(g_ffn, f32)[0]; g_final = np.asarray(g_final, f32)
    b_gate = np.asarray(b_gate, f32)[0]; sink = np.asarray(attn_sink, f32)[0]

    wmf = np.concatenate([w_mod[:, 0:2048], w_mod[:, 2048:4096], w_mod[:, 6144:8192], w_mod[:, 8192:10240]], 1)
    wmf_s = np.ascontiguousarray(wmf.reshape(16, 128, 32, 256).transpose(2, 1, 0, 3)).reshape(32, 128, 4096)
    wmg = np.concatenate([w_mod[:, 4096:6144], w_mod[:, 10240:12288]], 1)
    wd0 = _fmtA(w_down[0:3072])
    wd1 = _fmtA(w_down[3072:5632])
    wts = np.concatenate([_fmtA(w_in), _fmtA(w_gate), _fmtA(w_attn), _fmtA(w_out), _fmtA(w_four), _fmtA(w_up),
                          wd0, wd1, wmf_s, _fmtA(wmg)], 0)
    assert wts.shape[0] == NW, wts.shape
    bmf = np.concatenate([b_mod[0:2048], b_mod[2048:4096], b_mod[6144:8192], b_mod[8192:10240]])
    bmodT = np.ascontiguousarray(bmf.reshape(64, 128).T)
    bmodg = np.concatenate([b_mod[4096:6144], b_mod[10240:12288]])[None, :]
    bgate = np.ascontiguousarray(b_gate.reshape(32, 128).T)
    gvec = np.ascontiguousarray(np.stack([g_mix.reshape(16, 128).T, g_ffn.reshape(16, 128).T], 1))
    ident, cctab, dft_full, dft_halves, dists, identg, ccpos, altc = _const_tables()

    in_maps = []
    meta = []
    for i in range(NCORES):
        if i % 2 == 0:
            fa = (5 * i) // 2; fb = fa + 1; hc = fa + 2; off = 0
        else:
            hc = (5 * i) // 2; fa = hc + 1; fb = hc + 2; off = 1024
        meta.append((fa, fb, hc, off))
        xs = np.stack([X[fa], X[fb], np.roll(X[hc], -off, axis=0)], 0)
        c3 = np.stack([C[fa], C[fb], C[hc]], 0)
        c3t = np.ascontiguousarray(c3.reshape(3, 16, 128).transpose(2, 1, 0))
        dftc = np.concatenate([dft_full, dft_halves[i % 2]], 0)
        in_maps.append({
            "xs": np.ascontiguousarray(xs), "c3t": c3t, "wts": wts, "dft": dftc, "ident": ident,
            "btab": dists[i % 2], "identg": identg, "cctab": cctab, "ccpos": ccpos, "altc": altc, "gvec": gvec, "bmodT": bmodT, "bmodg": bmodg, "bgate": bgate,
            "sink": np.ascontiguousarray(sink[None, :]), "gfin": np.ascontiguousarray(g_final[None, :]),
        })
    return in_maps, meta


_PROGRAM = {}


def kernel(**inputs):
    in_maps, meta = prepare(**inputs)
    if "nc" not in _PROGRAM:
        _PROGRAM["nc"] = build_program()
    res = run_bass_kernel_spmd(_PROGRAM["nc"], in_maps, core_ids=list(range(NCORES)))
    Y = np.zeros((20, S, D), np.float32)
    for i in range(NCORES):
        fa, fb, hc, off = meta[i]
        yc = np.asarray(res.results[i]["y"], np.float32)
        Y[fa] = yc[0:2048]
        Y[fb] = yc[2048:4096]
        Y[hc, off:off + 1024] = yc[4096:5120]
    return Y[:4].copy(), Y[4:].copy()
```
